# Optimizing a Trainium2 kernel written in Bass

```python
import jax, jax.numpy as jnp
from jax import lax
import numpy as np

D_MODEL = 2048
BATCH = 8
SEQ = 4096
DEPTH = 2
DEC_BATCH = 2
DEC_SEQ = 4096
PAST_LEN = 128

HA = 8
HKV = 2
GROUP = HA // HKV
DH = 128
WA = HA * DH
WKV = HKV * DH
WINDOW = 128
BLOCK = 128
N_BUCKETS = 32
MAX_DIST = 128
HB = 4
DK = 128
DV = 256
WBK = HB * DK
WB = HB * DV
GATE_RANK = 16
GATE_NORM = 16.0
CHUNK = 64
SPLITS = [WA, WKV, WKV, WA, WBK, WBK, WB, WB, GATE_RANK, GATE_RANK]
D_IN = sum(SPLITS)
D_MIX = WA + WB
NEG = -1e30

kernel_name = "hymba_style_bidir_swa_gla_encoder"


def rms_norm(x, w, eps=1e-6):
    xf = x.astype(jnp.float32)
    y = xf * lax.rsqrt(jnp.mean(xf * xf, axis=-1, keepdims=True) + eps)
    return (y * w.astype(jnp.float32)).astype(x.dtype)


def _band_static():
    qi = np.arange(BLOCK)[:, None]
    kj = np.arange(3 * BLOCK)[None, :]
    rel = kj - BLOCK - qi
    nb = N_BUCKETS // 2
    max_exact = nb // 2
    n = np.abs(rel)
    large = max_exact + (np.log(np.maximum(n, 1) / max_exact) / np.log(MAX_DIST / max_exact)
                         * (nb - max_exact)).astype(np.int32)
    large = np.minimum(large, nb - 1)
    bucket = (rel > 0).astype(np.int32) * nb + np.where(n < max_exact, n, large)
    return bucket.astype(np.int32), (n <= WINDOW)


def windowed_gqa(q, k, v, rel_bias, sink):
    B, L = q.shape[0], q.shape[1]
    nb = L // BLOCK
    bucket, in_band = _band_static()
    pos_bias = jnp.transpose(rel_bias[bucket], (2, 0, 1)).astype(jnp.float32)
    pad = ((0, 0), (BLOCK, BLOCK), (0, 0), (0, 0))
    kp = jnp.pad(k, pad).reshape(B, nb + 2, BLOCK, HKV, DH)
    vp = jnp.pad(v, pad).reshape(B, nb + 2, BLOCK, HKV, DH)
    kw = jnp.concatenate([kp[:, :-2], kp[:, 1:-1], kp[:, 2:]], axis=2)
    vw = jnp.concatenate([vp[:, :-2], vp[:, 1:-1], vp[:, 2:]], axis=2)
    qb = q.reshape(B, nb, BLOCK, HKV, GROUP, DH)
    s = jnp.einsum('bnqhgd,bnkhd->bnhgqk', qb, kw).astype(jnp.float32) * (DH ** -0.5)
    s = s.reshape(B, nb, HA, BLOCK, 3 * BLOCK) + pos_bias
    key_pos = np.arange(nb)[:, None] * BLOCK + np.arange(3 * BLOCK)[None, :] - BLOCK
    valid = in_band[None] & ((key_pos >= 0) & (key_pos < L))[:, None, :]
    s = jnp.where(valid[None, :, None], s, NEG)
    sink_f = sink.astype(jnp.float32)[None, None, :, None, None]
    m = jnp.maximum(jnp.max(s, axis=-1, keepdims=True), sink_f)
    p = jnp.exp(s - m)
    p = p / (jnp.sum(p, axis=-1, keepdims=True) + jnp.exp(sink_f - m))
    p = p.astype(v.dtype).reshape(B, nb, HKV, GROUP, BLOCK, 3 * BLOCK)
    o = jnp.einsum('bnhgqk,bnkhd->bnqhgd', p, vw)
    return o.reshape(B, L, WA)


def gla_direction(q, k, v, g):
    B, L = q.shape[0], q.shape[1]
    nc = L // CHUNK
    q = q.reshape(B, nc, CHUNK, HB, DK)
    k = k.reshape(B, nc, CHUNK, HB, DK)
    v = v.reshape(B, nc, CHUNK, HB, DV)
    b = jnp.cumsum(g.reshape(B, nc, CHUNK, HB, DK), axis=2)
    b_last = b[:, :, -1:]
    q_dec = q * jnp.exp(b)
    k_dec = k * jnp.exp(-b)
    k_tail = k * jnp.exp(b_last - b)
    tri = np.tril(np.ones((CHUNK, CHUNK), dtype=bool))
    a = jnp.where(tri, jnp.einsum('bnihd,bnjhd->bnhij', q_dec, k_dec), 0.0)
    o_intra = jnp.einsum('bnhij,bnjhv->bnihv', a, v)
    kv = jnp.einsum('bnjhd,bnjhv->bnhdv', k_tail, v)
    decay = jnp.exp(b_last[:, :, 0])

    def step(S, inp):
        qd, dec, kvc = inp
        o = jnp.einsum('bihd,bhdv->bihv', qd, S)
        return S * dec[..., None] + kvc, o

    S0 = jnp.zeros((B, HB, DK, DV), jnp.float32)
    _, o_inter = lax.scan(step, S0, (jnp.moveaxis(q_dec, 1, 0), jnp.moveaxis(decay, 1, 0),
                                     jnp.moveaxis(kv, 1, 0)))
    o = o_intra + jnp.moveaxis(o_inter, 0, 1)
    return o.reshape(B, L, HB, DV)


def hybrid_layer(x, rel_bias, w_in, w_gk_f, b_gk_f, w_gk_b, b_gk_b, sink, gla_norm, w_out,
                 norm_pre, norm_post):
    B, L = x.shape[0], x.shape[1]
    h = rms_norm(x, norm_pre)
    proj = h @ w_in
    qa, ka, va, za, qb, kb, vb, zb, lrf, lrb = jnp.split(proj, list(np.cumsum(SPLITS)[:-1]), axis=-1)
    attn = windowed_gqa(qa.reshape(B, L, HA, DH), ka.reshape(B, L, HKV, DH),
                        va.reshape(B, L, HKV, DH), rel_bias, sink)
    attn = attn * jax.nn.silu(za)
    f32 = jnp.float32
    q_b = qb.astype(f32).reshape(B, L, HB, DK) * (DK ** -0.5)
    k_b = kb.astype(f32).reshape(B, L, HB, DK)
    v_b = vb.astype(f32).reshape(B, L, HB, DV)
    g_f = (jax.nn.log_sigmoid((lrf @ w_gk_f + b_gk_f).astype(f32)) / GATE_NORM).reshape(B, L, HB, DK)
    g_b = (jax.nn.log_sigmoid((lrb @ w_gk_b + b_gk_b).astype(f32)) / GATE_NORM).reshape(B, L, HB, DK)
    o_fwd = gla_direction(q_b, k_b, v_b, g_f)
    o_bwd = jnp.flip(gla_direction(jnp.flip(q_b, 1), jnp.flip(k_b, 1), jnp.flip(v_b, 1),
                                   jnp.flip(g_b, 1)), 1)
    o_gla = rms_norm((o_fwd + o_bwd).astype(x.dtype), gla_norm)
    o_gla = o_gla.reshape(B, L, WB) * jax.nn.silu(zb)
    mix = jnp.concatenate([attn, o_gla], axis=-1) @ w_out
    return x + rms_norm(mix, norm_post)


def setup_inputs(seed: int = 0) -> dict:
    key = jax.random.key(seed)
    ks = jax.random.split(key, 16)
    nrm = jax.random.normal
    return {
        "x_prompt": nrm(ks[0], (BATCH, SEQ, D_MODEL), jnp.float32),
        "x_sample": nrm(ks[1], (DEC_BATCH, DEC_SEQ, D_MODEL), jnp.float32),
        "rel_bias": 0.5 * nrm(ks[2], (N_BUCKETS, HA), jnp.float32),
        "w_in": nrm(ks[3], (DEPTH, D_MODEL, D_IN), jnp.float32) * D_MODEL ** -0.5,
        "w_gk_fwd": nrm(ks[4], (DEPTH, GATE_RANK, WBK), jnp.float32) * GATE_RANK ** -0.5,
        "b_gk_fwd": 0.1 * nrm(ks[5], (DEPTH, WBK), jnp.float32),
        "w_gk_bwd": nrm(ks[6], (DEPTH, GATE_RANK, WBK), jnp.float32) * GATE_RANK ** -0.5,
        "b_gk_bwd": 0.1 * nrm(ks[7], (DEPTH, WBK), jnp.float32),
        "sink": 0.5 * nrm(ks[8], (DEPTH, HA), jnp.float32),
        "gla_norm": 1.0 + 0.02 * nrm(ks[9], (DEPTH, DV), jnp.float32),
        "w_out": nrm(ks[10], (DEPTH, D_MIX, D_MODEL), jnp.float32) * D_MIX ** -0.5,
        "norm_pre": 1.0 + 0.02 * nrm(ks[11], (DEPTH, D_MODEL), jnp.float32),
        "norm_post": 1.0 + 0.02 * nrm(ks[12], (DEPTH, D_MODEL), jnp.float32),
    }


def reference(x_prompt, x_sample, rel_bias, w_in, w_gk_fwd, b_gk_fwd, w_gk_bwd, b_gk_bwd, sink,
              gla_norm, w_out, norm_pre, norm_post):
    y_prompt = x_prompt
    y_sample = x_sample
    for l in range(DEPTH):
        y_prompt = hybrid_layer(y_prompt, rel_bias, w_in[l], w_gk_fwd[l], b_gk_fwd[l], w_gk_bwd[l],
                                b_gk_bwd[l], sink[l], gla_norm[l], w_out[l], norm_pre[l], norm_post[l])
        y_sample = hybrid_layer(y_sample, rel_bias, w_in[l], w_gk_fwd[l], b_gk_fwd[l], w_gk_bwd[l],
                                b_gk_bwd[l], sink[l], gla_norm[l], w_out[l], norm_pre[l], norm_post[l])
    return (y_prompt, y_sample)
```

```python
import numpy as np
import ml_dtypes
import concourse.bass as bass
import concourse.mybir as mybir
from concourse.bass_utils import run_bass_kernel_spmd

F32 = mybir.dt.float32
BF16 = mybir.dt.bfloat16
AF = mybir.ActivationFunctionType
ALU = mybir.AluOpType

L = 4096
D = 2048
DIN = 5664
NS = 2
NLAYER = 2
EPS = 1e-6
ENGS = ["pe", "act", "dve", "pool", "sp"]
FOLD_WAIT = False
EIDX = {e: i for i, e in enumerate(ENGS)}


class Buf:
    __slots__ = ("name", "w", "r", "war", "sem", "cum", "uid")
    _n = [0]

    def __init__(self, name):
        Buf._n[0] += 1
        self.uid = Buf._n[0]
        self.name = name
        self.w = {}
        self.r = {}
        self.war = {}
        self.sem = None
        self.cum = 0


class Op:
    __slots__ = ("eng", "fn", "deps", "need_inc", "val", "sem", "is_dma", "order", "key", "seq", "clock")


class Prog:
    def __init__(self, nc, es):
        self.nc = nc
        self.es = es
        self.ops = {e: [] for e in ENGS}
        self.esem = {e: es.enter_context(nc.semaphore("sem_" + e)) for e in ENGS}
        self.order = 0
        self.last = {e: None for e in ENGS}
        self.pending_dma = {}
        self.nsem = len(ENGS)
        self.free_sems = []
        self.active_owners = []
        self.eclock = {e: [-1] * len(ENGS) for e in ENGS}
        self.eseq = {e: 0 for e in ENGS}

    def _mk(self, eng, fn, reads, writes, pwrites, owner, extra):
        op = Op()
        op.eng = eng
        op.fn = fn
        op.need_inc = False
        op.val = None
        op.sem = None
        op.is_dma = owner is not None
        self.order += 1
        op.order = self.order
        deps = {}

        ck = self.eclock[eng]

        def dep(o):
            if o is None:
                return
            if (not o.is_dma) and (not op.is_dma) and o.eng == "pe" and eng == "pe":
                return
            if (not o.is_dma) and ck[EIDX[o.eng]] >= o.seq:
                return
            cur = deps.get(o.key)
            if cur is None or o.order > cur.order:
                deps[o.key] = o

        for b in reads:
            for o in b.w.values():
                dep(o)
        for b in list(writes) + list(pwrites):
            if b.r:
                b.war = b.r
                b.r = {}
                b.w = {}
            for o in b.war.values():
                dep(o)
        for b in writes:
            for o in b.w.values():
                dep(o)
        for o in extra:
            dep(o)
        if op.is_dma:
            if owner.sem is None:
                if self.free_sems:
                    owner.sem, owner.cum = self.free_sems.pop()
                else:
                    owner.sem = self.es.enter_context(self.nc.semaphore("dsem%d" % self.nsem))
                    owner.cum = 0
                    self.nsem += 1
                self.active_owners.append(owner)
            owner.cum += 16
            op.sem = owner.sem
            op.val = owner.cum
            op.key = ("d", owner.uid)
            self.pending_dma[op.key] = op
        else:
            op.key = eng
        for o in deps.values():
            o.need_inc = True
            for i2 in range(len(ENGS)):
                if o.clock[i2] > ck[i2]:
                    ck[i2] = o.clock[i2]
            if (not o.is_dma) and o.seq > ck[EIDX[o.eng]]:
                ck[EIDX[o.eng]] = o.seq
        op.seq = self.eseq[eng]
        self.eseq[eng] += 1
        cl = list(ck)
        if not op.is_dma:
            cl[EIDX[eng]] = op.seq
        op.clock = tuple(cl)
        op.deps = list(deps.values())
        for b in reads:
            b.r[op.key] = op
        for b in writes:
            b.w = {op.key: op}
        for b in pwrites:
            b.w[op.key] = op
        self.ops[eng].append(op)
        self.last[eng] = op
        return op

    def add(self, eng, fn, reads=(), writes=(), pwrites=(), extra=()):
        return self._mk(eng, fn, reads, writes, pwrites, None, extra)

    def dma(self, eng, out, in_, owner, reads=(), writes=(), pwrites=(), **kw):
        return self._mk(eng, I("dma_start", out=out, in_=in_, **kw), reads, writes, pwrites, owner, ())

    def barrier(self):
        pend = list(self.pending_dma.values())
        self.pending_dma = {}
        n1 = []
        for e in ENGS:
            extra = list(pend)
            if self.last[e] is not None:
                extra.append(self.last[e])
            n1.append(self._mk(e, lambda en: en.nop(), (), (), (), None, extra))
        for e in ENGS:
            self._mk(e, lambda en: en.nop(), (), (), (), None, n1)
        for b in self.active_owners:
            self.free_sems.append((b.sem, b.cum))
            b.sem = None
        self.active_owners = []

    def final_wait(self):
        pend = list(self.pending_dma.values())
        self._mk("sp", lambda en: en.nop(), (), (), (), None, pend)
        self._mk("sp", lambda en: en.nop(), (), (), (), None, [self.last["sp"]])

    def emit(self, block):
        nc = self.nc
        counts = {}
        for e in ENGS:
            c = 0
            for op in self.ops[e]:
                if op.is_dma:
                    continue
                if op.need_inc:
                    c += 1
                    op.val = c
                    op.sem = self.esem[e]
            counts[e] = (len(self.ops[e]), c)
        self.counts = counts
        self.nwaits = {}

        def run(eng_name):
            def body(E):
                known = {}
                for op in self.ops[eng_name]:
                    need = []
                    for d in op.deps:
                        k = id(d.sem)
                        if known.get(k, 0) < d.val:
                            need.append((d.sem, d.val))
                            known[k] = d.val
                    fold = need.pop() if (need and FOLD_WAIT) else None
                    for (sm, vl) in need:
                        E.wait_ge(sm, vl)
                        self.nwaits[eng_name] = self.nwaits.get(eng_name, 0) + 1
                    ins = op.fn(E)
                    if fold is not None:
                        ins._wait_ge(fold[0], fold[1])
                    if op.is_dma:
                        ins.then_inc(op.sem, 16)
                    elif op.need_inc:
                        ins.then_inc(op.sem, 1)

            return body

        block.tensor(run("pe"))
        block.scalar(run("act"))
        block.vector(run("dve"))
        block.gpsimd(run("pool"))
        block.sync(run("sp"))


def I(name, *a, **k):
    def fn(e):
        return getattr(e, name)(*a, **k)
    return fn


class Arena:
    def __init__(self, ap, nwords):
        self.ap = ap
        self.n = nwords
        self.top = 0
        self.mark = 0

    def f32(self, n, name):
        n = (n + 7) // 8 * 8
        assert self.top + n <= self.n, ("arena overflow", name, self.top, n)
        a = self.ap[:, self.top:self.top + n]
        self.top += n
        return a, Buf(name)

    def bf16(self, n, name):
        w = (n + 1) // 2
        a, b = self.f32(w, name)
        return a.bitcast(BF16)[:, 0:n], b

    def set_mark(self):
        self.mark = self.top

    def reset(self):
        self.top = self.mark


class Rot:
    def __init__(self, items):
        self.items = items
        self.i = 0

    def next(self):
        it = self.items[self.i % len(self.items)]
        self.i += 1
        return it


def _consts():
    c = {}
    c["ident"] = np.eye(128, dtype=np.float32).astype(ml_dtypes.bfloat16)
    jj = np.arange(128)[:, None]
    ii = np.arange(128)[None, :]
    same = (jj // 64) == (ii // 64)
    g = -1.0 / 16.0
    tri = np.zeros((6, 128, 128), np.float32)
    tri[0] = np.where(same & (jj <= ii), g, 0.0)
    tri[1] = np.where(same & (jj >= ii), g, 0.0)
    tri[2] = np.where(same & (jj > ii), g, 0.0)
    tri[3] = np.where(same & (jj < ii), g, 0.0)
    tri[4] = np.where(same & (ii >= jj), 1.0, 0.0)
    tri[5] = np.where(same & (ii <= jj), 1.0, 0.0)
    c["tri"] = np.ascontiguousarray(tri.transpose(1, 0, 2))
    nb = 16
    max_exact = 8
    oh = np.zeros((32, 512), np.float32)
    vm = np.zeros((8, 512), np.float32)
    for idx in range(512):
        rel = 256 - idx
        n = abs(rel)
        if n > 128:
            continue
        if n < max_exact:
            v = n
        else:
            v = max_exact + int(np.log(max(n, 1) / max_exact) / np.log(128 / max_exact) * (nb - max_exact))
            v = min(v, nb - 1)
        b = (nb if rel > 0 else 0) + v
        oh[b, idx] = 1.0
        vm[:, idx] = 1.0
    c["oh"] = oh
    c["vm"] = vm
    c["ones_bf"] = np.ones((128, 128), np.float32).astype(ml_dtypes.bfloat16)
    c["ones_f"] = np.ones((128, 128), np.float32)
    c["J"] = np.ascontiguousarray(np.eye(128, dtype=np.float32)[::-1])
    return c


def _bucket_check():
    qi = np.arange(128)[:, None]
    kj = np.arange(384)[None, :]
    rel = kj - 128 - qi
    nb = 16
    max_exact = 8
    n = np.abs(rel)
    large = max_exact + (np.log(np.maximum(n, 1) / max_exact) / np.log(128 / max_exact) * (nb - max_exact)).astype(np.int32)
    large = np.minimum(large, nb - 1)
    bucket = (rel > 0).astype(np.int32) * nb + np.where(n < max_exact, n, large)
    return bucket, rel


def _blocks():
    blk = {j: {"fm": [], "tm": []} for j in range(12)}
    for c in range(4):
        blk[0]["fm"].append((c * 128, 128, "FB", c * 128, "cp"))
        blk[1]["fm"].append((c * 128, 128, "FB", 512 + c * 128, "cp"))
    blk[2]["fm"].append((0, 128, "FB", 1024, "cp"))
    blk[2]["fm"].append((128, 128, "FB", 1152, "cp"))
    blk[2]["tm"].append((256, 256, "TB", 0))
    for c in range(4):
        blk[3]["fm"].append((c * 128, 128, "FF", c * 128, "silu"))
        blk[4]["fm"].append((c * 128, 128, "FF", 512 + c * 128, "silu"))
        blk[5]["fm"].append((c * 128, 128, "FF", 1024 + c * 128, "cp"))
        blk[6]["fm"].append((c * 128, 128, "FF", 1536 + c * 128, "cp"))
        blk[9]["fm"].append((c * 128, 128, "FF", 2048 + c * 128, "silu"))
        blk[10]["fm"].append((c * 128, 128, "FF", 2560 + c * 128, "silu"))
    blk[6]["tm"].append((0, 512, "TF", 0))
    blk[7]["tm"].append((0, 512, "TB", 256))
    blk[8]["tm"].append((0, 512, "TB", 768))
    blk[11]["fm"].append((0, 32, "FF", 3072, "cp"))
    return blk


def build(cfg=None):
    cfg = dict(cfg or {})
    ns = cfg.get("ns", NS)
    nlayer = cfg.get("nlayer", NLAYER)
    phases = cfg.get("phases", "WABCD")
    ntA = cfg.get("ntA", 8)
    dbg = cfg.get("dbg", False)

    nc = bass.Bass("TRN2", target_bir_lowering=False)
    from contextlib import ExitStack
    es = ExitStack()
    es.enter_context(nc.allow_low_precision("bf16 matmul operands, fp32 accumulation"))

    def din(name, shape, dt=F32):
        return nc.dram_tensor(name, list(shape), dt, kind="ExternalInput").ap()

    def dscr(name, shape, dt, out=False):
        return nc.dram_tensor(name, list(shape), dt, kind=("ExternalOutput" if out else "Internal")).ap()

    x_in = din("x", [NS, L, D])
    rel_bias = din("rel_bias", [32, 8])
    w_in = din("w_in", [NLAYER, D, DIN])
    w_gk = [din("w_gk_fwd", [NLAYER, 16, 512]), din("w_gk_bwd", [NLAYER, 16, 512])]
    b_gk = [din("b_gk_fwd", [NLAYER, 512]), din("b_gk_bwd", [NLAYER, 512])]
    sink = din("sink", [NLAYER, 8])
    gla_norm = din("gla_norm", [NLAYER, 256])
    w_out = din("w_out", [NLAYER, D, D])
    norm_pre = din("norm_pre", [NLAYER, D])
    norm_post = din("norm_post", [NLAYER, D])
    c_ident = din("c_ident", [128, 128], BF16)
    c_tri = din("c_tri", [128, 6, 128])
    c_oh = din("c_oh", [32, 512])
    c_vm = din("c_vm", [8, 512])
    c_ones_bf = din("c_ones_bf", [128, 128], BF16)
    c_ones_f = din("c_ones_f", [128, 128])
    c_J = din("c_J", [128, 128])

    y_out = dscr("y", [NS, L, D], F32, out=True)
    so = dbg
    wbf = dscr("wbf", [NLAYER, D, DIN], BF16, out=False)
    woutbf = dscr("woutbf", [NLAYER, D, D], BF16, out=False)
    def shared(name, shape, dt):
        t = dscr(name + "0", shape, dt, out=so)
        return [t for _ in range(NS)]

    FB = shared("FB", [1280, L], BF16)
    FF = shared("FF", [3104, L], F32)
    TB = shared("TB", [L, 1280], BF16)
    TF = shared("TF", [L, 512], F32)
    MIX = shared("MIX", [D, L], BF16)
    OB = shared("OB", [1024, L], F32)
    if dbg:
        Y1 = [dscr(f"Y1{s}", [L, D], F32, out=so) for s in range(NS)]
    else:
        Y1 = [y_out[s] for s in range(NS)]
    UB = dscr("UB", [8, 512], F32, out=False)

    NW = 52000
    arena_t = es.enter_context(nc.sbuf_tensor("arena", [128, NW], F32))
    psum_t = es.enter_context(nc.psum_tensor("psum", [128, 4096], F32))
    A = Arena(arena_t, NW)
    P = Prog(nc, es)

    def bank(i, n=1):
        return psum_t[:, i * 512:(i + n) * 512]

    PSB = [Buf(f"psb{i}") for i in range(8)]

    b_wbf = [Buf(f"wbf{l}") for l in range(NLAYER)]
    b_wout = [Buf(f"woutbf{l}") for l in range(NLAYER)]
    def sharedb(name):
        b = Buf(name)
        return [b for _ in range(NS)]

    b_FB = sharedb("bFB")
    b_FF = sharedb("bFF")
    b_TB = sharedb("bTB")
    b_TF = sharedb("bTF")
    b_MIX = sharedb("bMIX")
    b_OB = sharedb("bOB")
    b_Y = [[Buf(f"bY{s}_{t}") for t in range(32)] for s in range(NS)]
    b_Y1 = [[Buf(f"bY1{s}_{t}") for t in range(32)] for s in range(NS)] if dbg else b_Y
    b_UB = Buf("bUB")

    ident, b_ident = A.bf16(128, "ident")
    tri, b_tri = A.f32(6 * 128, "tri")
    tri3 = tri.rearrange("p (a b) -> p a b", a=6)
    ones_bf, b_ones_bf = A.bf16(128, "ones_bf")
    ones_f, b_ones_f = A.f32(128, "ones_f")
    expbT, b_expbT = A.f32(3 * 8 * 128, "expbT")
    expbT4 = expbT.rearrange("p (o h q) -> p o h q", o=3, h=8)
    esink, b_esink = A.f32(8 * NLAYER, "esink")
    gnorm, b_gnorm = A.f32(2 * NLAYER, "gnorm")
    P.dma("sp", ident, c_ident, b_ident, writes=[b_ident])
    P.dma("sp", tri3, c_tri, b_tri, writes=[b_tri])
    P.dma("sp", ones_bf, c_ones_bf, b_ones_bf, writes=[b_ones_bf])
    P.dma("sp", ones_f, c_ones_f, b_ones_f, writes=[b_ones_f])
    for l in range(NLAYER):
        P.dma("sp", esink[:, l * 8:(l + 1) * 8], sink[l].partition_broadcast(128), b_esink, pwrites=[b_esink])
        for v in range(2):
            P.dma("sp", gnorm[:, l * 2 + v:l * 2 + v + 1],
                  gla_norm[l, v * 128:(v + 1) * 128].rearrange("(p o) -> p o", o=1), b_gnorm, pwrites=[b_gnorm])
    P.add("act", I("activation", out=esink, in_=esink, func=AF.Exp), reads=[b_esink], writes=[b_esink])
    A.set_mark()

    def phase0():
        A.reset()
        oh, b_oh = A.f32(512, "oh")
        vm, b_vm = A.f32(512, "vm")
        rb, b_rb = A.f32(8, "rb")
        u, b_u = A.f32(512, "u")
        P.dma("sp", oh[0:32, :], c_oh, b_oh, writes=[b_oh])
        P.dma("sp", vm[0:8, :], c_vm, b_vm, writes=[b_vm])
        P.dma("sp", rb[0:32, :], rel_bias, b_rb, writes=[b_rb])
        ps = bank(0)
        P.add("pe", I("matmul", ps[0:8, :], lhsT=rb[0:32, 0:8], rhs=oh[0:32, :], start=True, stop=True),
              reads=[b_rb, b_oh], writes=[PSB[0]])
        P.add("act", I("activation", out=u[0:8, :], in_=ps[0:8, :], func=AF.Exp), reads=[PSB[0]], writes=[b_u])
        P.add("dve", I("tensor_tensor", out=u[0:8, :], in0=u[0:8, :], in1=vm[0:8, :], op=ALU.mult),
              reads=[b_u, b_vm], writes=[b_u])
        P.dma("sp", UB, u[0:8, :], b_u, reads=[b_u], writes=[b_UB])
        ubt = UB.tensor
        jm, b_jm = A.f32(128, "jm")
        P.dma("sp", jm, c_J, b_jm, writes=[b_jm])
        for o in range(3):
            wt, b_wt = A.f32(1024, f"wt{o}")
            src_ap = bass.AP(tensor=ubt, offset=129 - (o - 1) * 128, ap=[[1, 128], [512, 8], [1, 128]])
            P.dma("sp", wt.rearrange("p (h q) -> p h q", h=8), src_ap, b_wt, reads=[b_UB], writes=[b_wt])
            for hf in range(2):
                pj = bank(1 + hf)
                P.add("pe", I("matmul", pj, lhsT=jm, rhs=wt[:, hf * 512:(hf + 1) * 512], start=True, stop=True),
                      reads=[b_jm, b_wt], writes=[PSB[1 + hf]])
                P.add("dve", I("tensor_copy", out=expbT4[:, o, hf * 4:(hf + 1) * 4, :],
                               in_=pj.rearrange("p (h q) -> p h q", h=4)), reads=[PSB[1 + hf]], pwrites=[b_expbT])
        P.barrier()

    def phaseW():
        lanes = [Buf(f"lane{i}") for i in range(4)]
        k = 0
        for l in range(nlayer):
            for r in range(8):
                src = w_in[l, r * 256:(r + 1) * 256, :].rearrange("r (a b) -> r a b", a=3)
                dst = wbf[l, r * 256:(r + 1) * 256, :].rearrange("r (a b) -> r a b", a=3)
                P.dma("pool", dst, src, lanes[k % 4], writes=[lanes[k % 4]], pwrites=[b_wbf[l]])
                k += 1
            for r in range(8):
                src = w_out[l, r * 256:(r + 1) * 256, :]
                dst = woutbf[l, r * 256:(r + 1) * 256, :]
                P.dma("pool", dst, src, lanes[k % 4], writes=[lanes[k % 4]], pwrites=[b_wout[l]])
                k += 1

    BLK = _blocks()

    def phaseA(l, s, xsrc, b_xsrc):
        A.reset()
        npre, b_npre = A.f32(D, "npre")
        P.dma("sp", npre, norm_pre[l].partition_broadcast(128), b_npre, writes=[b_npre])
        xs = Rot([A.f32(D, f"xs{i}") for i in range(2)])
        junk, b_junk = A.bf16(D, "junk")
        hb = Rot([A.bf16(D, f"hb{i}") for i in range(2)])
        hT = Rot([A.bf16(16 * 512, f"hT{i}") for i in range(2)])
        wb = Rot([A.bf16(16 * 512, f"wb{i}") for i in range(3)])
        stg = Rot([A.f32(512, f"stg{i}") for i in range(6)])
        small = Rot([A.f32(8, f"sm{i}") for i in range(4)])
        pt = psum_t[:, 0:1024].bitcast(BF16)
        po = Rot([2, 3, 4, 5, 6, 7])
        dst_ap = {"FB": FB[s], "FF": FF[s], "TB": TB[s], "TF": TF[s]}
        dst_buf = {"FB": b_FB[s], "FF": b_FF[s], "TB": b_TB[s], "TF": b_TF[s]}
        ncopy = [0]

        def evac(psap, mode, dt_bf, shape_p, n):
            sa, sb = stg.next()
            if dt_bf:
                o = sa.bitcast(BF16)[0:shape_p, 0:n]
            else:
                o = sa[0:shape_p, 0:n]
            return o, sb

        hT_of = {}

        def front_ew(T, i):
            if T not in hT_of:
                hTa, b_hT = hT.next()
                hT_of[T] = (hTa.rearrange("p (k t) -> p k t", k=16), b_hT)
            xa, b_x = xs.next()
            r0 = (T * 4 + i) * 128
            P.dma("sp", xa, xsrc[r0:r0 + 128, :], b_x, reads=[b_xsrc[T * 4 + i]], writes=[b_x])
            sm, b_sm = small.next()
            P.add("act", I("activation", out=junk, in_=xa, func=AF.Square, accum_out=sm[:, 0:1]),
                  reads=[b_x], writes=[b_junk, b_sm])
            P.add("dve", I("tensor_scalar", out=sm[:, 1:2], in0=sm[:, 0:1], scalar1=1.0 / D, scalar2=EPS,
                           op0=ALU.mult, op1=ALU.add), reads=[b_sm], writes=[b_sm])
            P.add("act", I("activation", out=sm[:, 3:4], in_=sm[:, 1:2], func=AF.Ln), reads=[b_sm], writes=[b_sm])
            P.add("act", I("activation", out=sm[:, 2:3], in_=sm[:, 3:4], func=AF.Exp, scale=-0.5), reads=[b_sm], writes=[b_sm])
            ha, b_h = hb.next()
            P.add("dve", I("scalar_tensor_tensor", out=ha, in0=xa, scalar=sm[:, 2:3], in1=npre, op0=ALU.mult, op1=ALU.mult),
                  reads=[b_x, b_sm, b_npre], writes=[b_h])
            return (T, i, ha, b_h)

        def front_pe(ctx):
            T, i, ha, b_h = ctx
            hT3, b_hT = hT_of[T]
            for kc in range(16):
                P.add("pe", I("transpose", pt[:, kc * 128:(kc + 1) * 128], ha[:, kc * 128:(kc + 1) * 128], ident),
                      reads=[b_h, b_ident], pwrites=[PSB[kc // 8]])
            for hf in range(2):
                src_ = pt[:, hf * 1024:(hf + 1) * 1024].rearrange("p (k t) -> p k t", k=8)
                dstv = hT3[:, hf * 8:(hf + 1) * 8, i * 128:(i + 1) * 128]
                if hf == 0:
                    P.add("act", I("copy", out=dstv, in_=src_), reads=[PSB[hf]], pwrites=[b_hT])
                else:
                    P.add("dve", I("tensor_copy", out=dstv, in_=src_), reads=[PSB[hf]], pwrites=[b_hT])

        def groups(T):
            hT3, b_hT = hT_of[T]
            for j in range(12):
                wa, b_w = wb.next()
                w3 = wa.rearrange("p (k c) -> p k c", k=16)
                ncol = 512 if j < 11 else 32
                for hk in range(2):
                    P.dma("sp", w3[:, hk * 8:(hk + 1) * 8, 0:ncol],
                          wbf[l, hk * 1024:(hk + 1) * 1024, j * 512:j * 512 + ncol].rearrange("(k p) c -> p k c", p=128),
                          b_w, reads=[b_wbf[l]], pwrites=[b_w])
                for (c0, ncl, dname, drow, mode) in BLK[j]["fm"]:
                    pb = po.next()
                    ps = bank(pb)
                    for kc in range(16):
                        P.add("pe", I("matmul", ps[0:ncl, :], lhsT=w3[:, kc, c0:c0 + ncl], rhs=hT3[:, kc, :], start=(kc == 0), stop=(kc == 15)),
                            reads=[b_w, b_hT], writes=([PSB[pb]] if kc == 0 else []), pwrites=([] if kc == 0 else [PSB[pb]]))
                    isbf = (dname == "FB")
                    o, b_o = evac(ps, mode, isbf, ncl, 512)
                    if mode == "silu":
                        P.add("act", I("activation", out=o, in_=ps[0:ncl, :], func=AF.Silu), reads=[PSB[pb]], writes=[b_o])
                    else:
                        ncopy[0] += 1
                        if ncopy[0] % 3 == 0:
                            P.add("act", I("copy", out=o, in_=ps[0:ncl, :]), reads=[PSB[pb]], writes=[b_o])
                        else:
                            P.add("dve", I("tensor_copy", out=o, in_=ps[0:ncl, :]), reads=[PSB[pb]], writes=[b_o])
                    P.dma("pool", dst_ap[dname][drow:drow + ncl, T * 512:(T + 1) * 512], o, b_o, reads=[b_o], pwrites=[dst_buf[dname]])
                    yield
                for (c0, ncl, dname, dcol) in BLK[j]["tm"]:
                    for i in range(4):
                        pb = po.next()
                        ps = bank(pb)
                        for kc in range(16):
                            P.add("pe", I("matmul", ps[:, 0:ncl], lhsT=hT3[:, kc, i * 128:(i + 1) * 128], rhs=w3[:, kc, c0:c0 + ncl],
                                start=(kc == 0), stop=(kc == 15)),
                                reads=[b_w, b_hT], writes=([PSB[pb]] if kc == 0 else []), pwrites=([] if kc == 0 else [PSB[pb]]))
                        isbf = (dname == "TB")
                        o, b_o = evac(ps, "cp", isbf, 128, ncl)
                        ncopy[0] += 1
                        if ncopy[0] % 3 == 0:
                            P.add("act", I("copy", out=o, in_=ps[:, 0:ncl]), reads=[PSB[pb]], writes=[b_o])
                        else:
                            P.add("dve", I("tensor_copy", out=o, in_=ps[:, 0:ncl]), reads=[PSB[pb]], writes=[b_o])
                        r0 = (T * 4 + i) * 128
                        P.dma("pool", dst_ap[dname][r0:r0 + 128, dcol:dcol + ncl], o, b_o, reads=[b_o], pwrites=[dst_buf[dname]])
                        yield

        for i in range(4):
            front_pe(front_ew(0, i))
        for T in range(ntA):
            pend = {}
            g = 0
            for _ in groups(T):
                g += 1
                if T + 1 < ntA:
                    for i in range(4):
                        if g == 4 + 11 * i:
                            pend[i] = front_ew(T + 1, i)
                        if g == 9 + 11 * i:
                            front_pe(pend.pop(i))
            assert not pend
        P.barrier()

    def phaseB(l, s):
        A.reset()
        kT, b_kT = A.bf16(2 * L, "kT")
        kT3 = kT.rearrange("p (g t) -> p g t", g=2)
        V, b_V = A.bf16(32 * 256, "V")
        V3 = V.rearrange("p (b c) -> p b c", b=32)
        P.dma("sp", kT3, FB[s][1024:1280, :].rearrange("(g p) t -> p g t", p=128), b_kT, reads=[b_FB[s]], writes=[b_kT])
        for part in range(4):
            P.dma("sp", V3[:, part * 8:(part + 1) * 8, :],
                  TB[s][part * 1024:(part + 1) * 1024, 0:256].rearrange("(b p) c -> p b c", p=128), b_V,
                  reads=[b_TB[s]], pwrites=[b_V])
        qT = Rot([A.bf16(8 * 512, f"qT{i}") for i in range(2)])
        sza = Rot([A.f32(8 * 512, f"sza{i}") for i in range(2)])
        atb = Rot([A.bf16(8 * 512, f"atb{i}") for i in range(2)])
        ef = Rot([A.f32(512, f"ef{i}") for i in range(4)])
        pT = Rot([A.bf16(512, f"pT{i}") for i in range(6)])
        rec = Rot([A.f32(512, f"rec{i}") for i in range(2)])
        at = Rot([A.f32(512, f"at{i}") for i in range(2)])
        stb = Rot([0, 1, 2, 3])
        otb = Rot([4, 5])
        dnb = Rot([6, 7])
        scale = 128 ** -0.5
        es_l = esink[:, l * 8:(l + 1) * 8]
        esrow, b_esrow = A.bf16(1024, "esrow")
        P.add("dve", I("tensor_copy", out=esrow[0:1, :].rearrange("p (h q) -> p h q", h=8),
                       in_=es_l[0:1, :].unsqueeze(2).to_broadcast([1, 8, 128])), reads=[b_esink], writes=[b_esrow])
        for T in range(8):
            qa, b_q = qT.next()
            q3 = qa.rearrange("p (h t) -> p h t", h=8)
            za, b_z = sza.next()
            z3 = za.rearrange("p (h t) -> p h t", h=8)
            aa, b_a = atb.next()
            a3 = aa.rearrange("p (h t) -> p h t", h=8)
            P.dma("sp", q3, FB[s][0:1024, T * 512:(T + 1) * 512].rearrange("(h p) t -> p h t", p=128), b_q,
                  reads=[b_FB[s]], writes=[b_q])
            P.dma("sp", z3, FF[s][0:1024, T * 512:(T + 1) * 512].rearrange("(h p) t -> p h t", p=128), b_z,
                  reads=[b_FF[s]], writes=[b_z])
            units = [(ib, g) for ib in range(4) for g in range(2)]
            pend = None

            def front(ib, g):
                qi = T * 4 + ib
                offs = [o for o in (-1, 0, 1) if 0 <= qi + o < 32]
                pts = []
                for o in offs:
                    kb = qi + o
                    sbk = stb.next()
                    st = bank(sbk)
                    P.add("pe", I("matmul",
                        st, lhsT=kT3[:, g, kb * 128:(kb + 1) * 128], rhs=q3[:, 4 * g:4 * g + 4, ib * 128:(ib + 1) * 128],
                        start=True, stop=True), reads=[b_kT, b_q], writes=[PSB[sbk]])
                    ea, b_e = ef.next()
                    P.add("act", I("activation", out=ea, in_=st, func=AF.Exp, scale=scale),
                          reads=[PSB[sbk]], writes=[b_e])
                    pa, b_p = pT.next()
                    P.add(("dve" if o == 0 else "pool"), I("tensor_tensor",
                        out=pa.rearrange("p (h q) -> p h q", h=4), in0=ea.rearrange("p (h q) -> p h q", h=4),
                        in1=expbT4[:, o + 1, 4 * g:4 * g + 4, :], op=ALU.mult), reads=[b_e, b_expbT], writes=[b_p])
                    pts.append((kb, pa, b_p))
                return (ib, g, pts)

            def back(u):
                ib, g, pts = u
                ob_ = otb.next()
                db_ = dnb.next()
                oT = bank(ob_)
                dn = bank(db_)
                n = len(pts)
                for k, (kb, pa, b_p) in enumerate(pts):
                    P.add("pe", I("matmul", oT, lhsT=V3[:, kb, g * 128:(g + 1) * 128], rhs=pa, start=(k == 0), stop=(k == n - 1)),
                        reads=[b_V, b_p], writes=([PSB[ob_]] if k == 0 else []), pwrites=([] if k == 0 else [PSB[ob_]]))
                P.add("pe", I("matmul", dn, lhsT=ones_bf[0:1, :], rhs=esrow[0:1, g * 512:(g + 1) * 512], start=True, stop=False),
                      reads=[b_ones_bf, b_esrow], writes=[PSB[db_]])
                for k, (kb, pa, b_p) in enumerate(pts):
                    P.add("pe", I("matmul", dn, lhsT=ones_bf, rhs=pa, start=False, stop=(k == n - 1)),
                        reads=[b_ones_bf, b_p], pwrites=[PSB[db_]])
                ra, b_r = rec.next()
                P.add("dve", I("reciprocal", out=ra, in_=dn), reads=[PSB[db_]], writes=[b_r])
                ta, b_t = at.next()
                P.add("pool", I("tensor_tensor", out=ta.rearrange("p (h q) -> p h q", h=4), in0=ra.rearrange("p (h q) -> p h q", h=4),
                    in1=z3[:, 4 * g:4 * g + 4, ib * 128:(ib + 1) * 128], op=ALU.mult), reads=[b_r, b_z], writes=[b_t])
                P.add("dve", I("tensor_tensor", out=a3[:, 4 * g:4 * g + 4, ib * 128:(ib + 1) * 128],
                    in0=oT.rearrange("p (h q) -> p h q", h=4), in1=ta.rearrange("p (h q) -> p h q", h=4), op=ALU.mult),
                    reads=[PSB[ob_], b_t], pwrites=[b_a])

            for (ib, g) in units:
                u = front(ib, g)
                if pend is not None:
                    back(pend)
                pend = u
            back(pend)
            P.dma("pool", MIX[s][0:1024, T * 512:(T + 1) * 512].rearrange("(h p) t -> p h t", p=128), a3, b_a,
                  reads=[b_a], pwrites=[b_MIX[s]])
        P.barrier()

    def phaseC(l, s, d):
        A.reset()
        wg, b_wg = A.f32(512, "wg")
        P.dma("sp", wg[0:16, :], w_gk[d][l], b_wg, pwrites=[b_wg])
        P.dma("sp", wg[16:17, :], b_gk[d][l:l + 1, :], b_wg, pwrites=[b_wg])
        lr = Rot([A.f32(512, f"lr{i}") for i in range(2)])
        for (a_, b_) in lr.items:
            P.add("pool", I("memset", a_[0:17, :], 1.0), writes=[b_])
        Sst, b_S = A.f32(1024, "S")
        S3 = Sst.rearrange("p (h v) -> p h v", h=4)
        P.add("pool", I("memset", Sst, 0.0), writes=[b_S])
        Sb = Rot([A.bf16(1024, f"Sb{i}") for i in range(2)])
        sb_cur, b_sbcur = Sb.next()
        P.add("pool", I("memset", sb_cur, 0.0), writes=[b_sbcur])
        qbT = Rot([A.f32(4 * 512, f"qbT{i}") for i in range(2)])
        kbT = Rot([A.f32(4 * 512, f"kbT{i}") for i in range(2)])
        kbk = Rot([A.f32(4 * 512, f"kbk{i}") for i in range(2)])
        vbk = Rot([A.bf16(4 * 1024, f"vbk{i}") for i in range(2)])
        if d == 0:
            obl = Rot([A.f32(8 * 128, f"obl{i}") for i in range(2)])
            szb = Rot([A.f32(8 * 128, f"szb{i}") for i in range(2)])
            mixb = Rot([A.bf16(8 * 512, f"mixb{i}") for i in range(2)])
        else:
            obs = Rot([A.f32(8 * 512, f"obs{i}") for i in range(2)])
        e1 = Rot([A.f32(512, f"e1{i}") for i in range(2)])
        spb = Rot([A.f32(512, f"sp{i}") for i in range(2)])
        EbT = Rot([A.f32(512, f"EbT{i}") for i in range(2)])
        EnbT = Rot([A.f32(512, f"EnbT{i}") for i in range(2)])
        Etl = Rot([A.f32(512, f"Etl{i}") for i in range(2)])
        qd = Rot([A.bf16(512, f"qd{i}") for i in range(2)])
        kd = Rot([A.bf16(512, f"kd{i}") for i in range(2)])
        ktl = Rot([A.bf16(512, f"ktl{i}") for i in range(2)])
        ATb = Rot([A.bf16(512, f"ATb{i}") for i in range(2)])
        of = Rot([A.f32(1024, f"of{i}") for i in range(2)])
        sq = Rot([A.f32(1024, f"sq{i}") for i in range(2)])
        rs = Rot([A.f32(512, f"rs{i}") for i in range(2)])
        TRI = tri3[:, (1 if d else 0), :]
        TT = tri3[:, (3 if d else 2), :]
        MSK = tri3[:, (5 if d else 4), :]
        P_GATE, P_BT, P_TAIL, P_AT, P_OT, P_KV = 0, 1, 2, 3, 4, 6
        gl = gnorm[:, l * 2:(l + 1) * 2]
        dk_scale = 128 ** -0.5
        torder = list(range(8)) if d == 0 else list(range(7, -1, -1))
        for T in torder:
            lra, b_lr = lr.next()
            P.dma("sp", lra[0:16, :], FF[s][3072 + 16 * d:3072 + 16 * d + 16, T * 512:(T + 1) * 512], b_lr,
                  reads=[b_FF[s]], pwrites=[b_lr])
            qa, b_q = qbT.next()
            q3 = qa.rearrange("p (h t) -> p h t", h=4)
            P.dma("sp", q3, FF[s][1024:1536, T * 512:(T + 1) * 512].rearrange("(h p) t -> p h t", p=128), b_q,
                  reads=[b_FF[s]], writes=[b_q])
            ka, b_k = kbT.next()
            k3 = ka.rearrange("p (h t) -> p h t", h=4)
            P.dma("sp", k3, FF[s][1536:2048, T * 512:(T + 1) * 512].rearrange("(h p) t -> p h t", p=128), b_k,
                  reads=[b_FF[s]], writes=[b_k])
            kka, b_kk = kbk.next()
            kk3 = kka.rearrange("p (i c) -> p i c", i=4)
            P.dma("sp", kk3, TF[s][T * 512:(T + 1) * 512, :].rearrange("(i p) c -> p i c", p=128), b_kk,
                  reads=[b_TF[s]], writes=[b_kk])
            va, b_v = vbk.next()
            v3 = va.rearrange("p (i c) -> p i c", i=4)
            P.dma("sp", v3, TB[s][T * 512:(T + 1) * 512, 256:1280].rearrange("(i p) c -> p i c", p=128), b_v,
                  reads=[b_TB[s]], writes=[b_v])
            if d == 0:
                mxa, b_mx = mixb.next()
                mx3 = mxa.rearrange("p (c t) -> p c t", c=8)
            else:
                osa, b_os = obs.next()
                os3 = osa.rearrange("p (c t) -> p c t", c=8)
            iorder = list(range(4)) if d == 0 else list(range(3, -1, -1))
            for i in iorder:
                tk = slice(i * 128, (i + 1) * 128)
                if d == 0:
                    t0 = T * 512 + i * 128
                    oba, b_ob = obl.next()
                    ob3 = oba.rearrange("p (c t) -> p c t", c=8)
                    P.dma("sp", ob3, OB[s][:, t0:t0 + 128].rearrange("(c p) t -> p c t", p=128), b_ob,
                          reads=[b_OB[s]], writes=[b_ob])
                    sza_, b_sz = szb.next()
                    sz3 = sza_.rearrange("p (c t) -> p c t", c=8)
                    P.dma("sp", sz3, FF[s][2048:3072, t0:t0 + 128].rearrange("(c p) t -> p c t", p=128), b_sz,
                          reads=[b_FF[s]], writes=[b_sz])
                pg = bank(P_GATE)
                P.add("pe", I("matmul", pg, lhsT=lra[0:17, tk], rhs=wg[0:17, :], start=True, stop=True),
                      reads=[b_lr, b_wg], writes=[PSB[P_GATE]])
                e1a, b_e1 = e1.next()
                P.add("act", I("activation", out=e1a, in_=pg, func=AF.Exp, scale=-1.0),
                      reads=[PSB[P_GATE]], writes=[b_e1])
                spa, b_sp = spb.next()
                P.add("act", I("activation", out=spa, in_=e1a, func=AF.Ln, bias=1.0),
                      reads=[b_e1], writes=[b_sp])
                pbt = bank(P_BT)
                for h in range(4):
                    P.add("pe", I("matmul",
                        pbt[:, h * 128:(h + 1) * 128], lhsT=spa[:, h * 128:(h + 1) * 128], rhs=TRI, start=True, stop=True),
                        reads=[b_sp, b_tri], writes=([PSB[P_BT]] if h == 0 else []), pwrites=([] if h == 0 else [PSB[P_BT]]))
                ptl = bank(P_TAIL)
                P.add("pe", I("matmul", ptl, lhsT=TT, rhs=spa, start=True, stop=True),
                      reads=[b_sp, b_tri], writes=[PSB[P_TAIL]])
                eb, b_eb = EbT.next()
                enb, b_enb = EnbT.next()
                etl, b_etl = Etl.next()
                P.add("act", I("activation", out=eb, in_=pbt, func=AF.Exp), reads=[PSB[P_BT]], writes=[b_eb])
                P.add("act", I("activation", out=enb, in_=pbt, func=AF.Exp, scale=-1.0),
                      reads=[PSB[P_BT]], writes=[b_enb])
                P.add("act", I("activation", out=etl, in_=ptl, func=AF.Exp), reads=[PSB[P_TAIL]], writes=[b_etl])
                eb3 = eb.rearrange("p (h t) -> p h t", h=4)
                qda, b_qd = qd.next()
                qd3 = qda.rearrange("p (h t) -> p h t", h=4)
                P.add("dve", I("scalar_tensor_tensor",
                    out=qd3, in0=q3[:, :, tk], scalar=dk_scale, in1=eb3, op0=ALU.mult, op1=ALU.mult),
                    reads=[b_q, b_eb], writes=[b_qd])
                kda, b_kd = kd.next()
                kd3 = kda.rearrange("p (h t) -> p h t", h=4)
                P.add("pool", I("tensor_tensor",
                    out=kd3, in0=k3[:, :, tk], in1=enb.rearrange("p (h t) -> p h t", h=4), op=ALU.mult),
                    reads=[b_k, b_enb], writes=[b_kd])
                kta, b_kt = ktl.next()
                P.add("pool", I("tensor_tensor",
                    out=kta, in0=kk3[:, i, :], in1=etl, op=ALU.mult), reads=[b_kk, b_etl], writes=[b_kt])
                pat = bank(P_AT)
                for h in range(4):
                    P.add("pe", I("matmul",
                        pat[:, h * 128:(h + 1) * 128], lhsT=kd3[:, h, :], rhs=qd3[:, h, :], start=True, stop=True),
                        reads=[b_kd, b_qd], writes=([PSB[P_AT]] if h == 0 else []), pwrites=([] if h == 0 else [PSB[P_AT]]))
                aba, b_ab = ATb.next()
                ab3 = aba.rearrange("p (h t) -> p h t", h=4)
                P.add("dve", I("tensor_tensor",
                    out=ab3, in0=pat.rearrange("p (h t) -> p h t", h=4), in1=MSK.unsqueeze(1).to_broadcast([128, 4, 128]),
                    op=ALU.mult), reads=[PSB[P_AT], b_tri], writes=[b_ab])
                pot = bank(P_OT, 2)
                corder = [0, 1] if d == 0 else [1, 0]
                first_in_bank = {0: True, 1: True}
                for h in range(4):
                    for v in range(2):
                        c8 = h * 2 + v
                        bk = c8 // 4
                        P.add("pe", I("matmul",
                            pot[:, c8 * 128:(c8 + 1) * 128], lhsT=v3[:, i, h * 256 + v * 128:h * 256 + (v + 1) * 128],
                            rhs=ab3[:, h, :], start=first_in_bank[bk], stop=False, skip_group_check=True),
                            reads=[b_v, b_ab], writes=([PSB[P_OT + bk]] if first_in_bank[bk] else []),
                            pwrites=([] if first_in_bank[bk] else [PSB[P_OT + bk]]))
                        first_in_bank[bk] = False
                for ci, c in enumerate(corder):
                    cs = slice(c * 64, (c + 1) * 64)
                    sbv = sb_cur.rearrange("p (h v) -> p h v", h=4)
                    for h in range(4):
                        for v in range(2):
                            c8 = h * 2 + v
                            bk = c8 // 4
                            P.add("pe", I("matmul",
                                pot[:, c8 * 128 + c * 64:c8 * 128 + (c + 1) * 64], lhsT=sbv[:, h, v * 128:(v + 1) * 128],
                                rhs=qd3[:, h, c * 64:(c + 1) * 64], start=False, stop=(ci == 1), skip_group_check=True),
                                reads=[b_sbcur, b_qd], pwrites=[PSB[P_OT + bk]])
                    pkv = bank(P_KV, 2)
                    for h in range(4):
                        bk = h // 2
                        P.add("pe", I("matmul",
                            pkv[:, h * 256:(h + 1) * 256], lhsT=kta[cs, h * 128:(h + 1) * 128], rhs=v3[cs, i, h * 256:(h + 1) * 256],
                            start=True, stop=True), reads=[b_kt, b_v],
                            writes=([PSB[P_KV + bk]] if h % 2 == 0 else []), pwrites=([] if h % 2 == 0 else [PSB[P_KV + bk]]))
                    if d == 0:
                        dcol = 63 if c == 0 else 127
                    else:
                        dcol = 64 if c == 1 else 0
                    for h in range(4):
                        P.add("dve", I("scalar_tensor_tensor",
                            out=S3[:, h, :], in0=S3[:, h, :], scalar=eb3[:, h, dcol:dcol + 1], in1=pkv[:, h * 256:(h + 1) * 256],
                            op0=ALU.mult, op1=ALU.add), reads=[b_S, b_eb, PSB[P_KV + h // 2]], writes=[b_S])
                    sb_cur, b_sbcur = Sb.next()
                    P.add("act", I("copy", out=sb_cur, in_=Sst), reads=[b_S], writes=[b_sbcur])
                if d == 1:
                    P.add("act", I("copy", out=os3[:, :, tk], in_=pot.rearrange("p (c t) -> p c t", c=8)),
                          reads=[PSB[P_OT], PSB[P_OT + 1]], pwrites=[b_os])
                else:
                    ofa, b_of = of.next()
                    of3 = ofa.rearrange("p (c t) -> p c t", c=8)
                    P.add("dve", I("tensor_tensor",
                        out=of3, in0=pot.rearrange("p (c t) -> p c t", c=8), in1=ob3, op=ALU.add),
                        reads=[PSB[P_OT], PSB[P_OT + 1], b_ob], writes=[b_of])
                    sqa, b_sq = sq.next()
                    P.add("act", I("activation", out=sqa, in_=ofa, func=AF.Square), reads=[b_of], writes=[b_sq])
                    pss = bank(P_GATE)
                    for h in range(4):
                        for v in range(2):
                            c8 = h * 2 + v
                            P.add("pe", I("matmul",
                                pss[:, h * 128:(h + 1) * 128], lhsT=ones_f, rhs=sqa[:, c8 * 128:(c8 + 1) * 128],
                                start=(h == 0 and v == 0), stop=(v == 1), skip_group_check=True),
                                reads=[b_sq, b_ones_f], writes=([PSB[P_GATE]] if c8 == 0 else []), pwrites=([] if c8 == 0 else [PSB[P_GATE]]))
                    rsa, b_rs = rs.next()
                    P.add("dve", I("tensor_scalar", out=rsa, in0=pss, scalar1=1.0 / 256, scalar2=EPS,
                                                                             op0=ALU.mult, op1=ALU.add), reads=[PSB[P_GATE]], writes=[b_rs])
                    P.add("act", I("activation", out=rsa, in_=rsa, func=AF.Ln), reads=[b_rs], writes=[b_rs])
                    P.add("act", I("activation", out=rsa, in_=rsa, func=AF.Exp, scale=-0.5), reads=[b_rs], writes=[b_rs])
                    P.add("dve", I("tensor_tensor",
                        out=ofa.rearrange("p (h v t) -> p h v t", h=4, v=2), in0=ofa.rearrange("p (h v t) -> p h v t", h=4, v=2),
                        in1=rsa.rearrange("p (h t) -> p h t", h=4).unsqueeze(2).to_broadcast([128, 4, 2, 128]), op=ALU.mult),
                        reads=[b_of, b_rs], writes=[b_of])
                    for v in range(2):
                        o4 = ofa.rearrange("p (h v t) -> p h v t", h=4, v=2)[:, :, v, :]
                        z4 = sz3.rearrange("p (h v) t -> p h v t", v=2)[:, :, v, :]
                        m4 = mx3[:, :, tk].rearrange("p (h v) t -> p h v t", v=2)[:, :, v, :]
                        P.add("dve", I("scalar_tensor_tensor",
                            out=m4, in0=o4, scalar=gl[:, v:v + 1], in1=z4, op0=ALU.mult, op1=ALU.mult),
                            reads=[b_of, b_gnorm, b_sz], pwrites=[b_mx])
            if d == 1:
                P.dma("pool", OB[s][:, T * 512:(T + 1) * 512].rearrange("(c p) t -> p c t", p=128), os3, b_os,
                      reads=[b_os], pwrites=[b_OB[s]])
            else:
                P.dma("pool", MIX[s][1024:2048, T * 512:(T + 1) * 512].rearrange("(c p) t -> p c t", p=128), mx3, b_mx,
                      reads=[b_mx], pwrites=[b_MIX[s]])
        P.barrier()

    def phaseD(l, s, xsrc, b_xsrc, ydst, b_ydst):
        A.reset()
        wo, b_wo = A.bf16(16 * D, "wo")
        wo3 = wo.rearrange("p (k c) -> p k c", k=16)
        b_won = [Buf(f"wo_n{n}") for n in range(4)]
        for n in range(4):
            for hk in range(2):
                P.dma("sp", wo3[:, hk * 8:(hk + 1) * 8, n * 512:(n + 1) * 512],
                      woutbf[l, hk * 1024:(hk + 1) * 1024, n * 512:(n + 1) * 512].rearrange("(k p) c -> p k c", p=128),
                      b_won[n], reads=[b_wout[l]], pwrites=[b_won[n]])
        npost, b_np = A.f32(D, "npost")
        P.dma("sp", npost, norm_post[l].partition_broadcast(128), b_np, writes=[b_np])
        mT = Rot([A.bf16(16 * 512, f"mT{i}") for i in range(2)])
        xs = Rot([A.f32(D, f"xd{i}") for i in range(2)])
        tmp = Rot([A.f32(D, f"tmp{i}") for i in range(2)])
        yb = Rot([A.f32(D, f"yb{i}") for i in range(2)])
        junk, b_junk = A.bf16(D, "junkd")
        small = Rot([A.f32(8, f"smd{i}") for i in range(4)])
        pgrp = Rot([0, 4])
        for T in range(ntA):
            ma, b_m = mT.next()
            m3 = ma.rearrange("p (k t) -> p k t", k=16)
            for hk in range(2):
                P.dma("sp", m3[:, hk * 8:(hk + 1) * 8, :],
                      MIX[s][hk * 1024:(hk + 1) * 1024, T * 512:(T + 1) * 512].rearrange("(k p) t -> p k t", p=128), b_m,
                      reads=[b_MIX[s]], pwrites=[b_m])
            for i in range(4):
                r0 = (T * 4 + i) * 128
                xa, b_x = xs.next()
                P.dma("sp", xa, xsrc[r0:r0 + 128, :], b_x, reads=[b_xsrc[T * 4 + i]], writes=[b_x])
                pg0 = pgrp.next()
                pso = bank(pg0, 4)
                for n in range(4):
                    for kc in range(16):
                        P.add("pe", I("matmul",
                            pso[:, n * 512:(n + 1) * 512], lhsT=m3[:, kc, i * 128:(i + 1) * 128], rhs=wo3[:, kc, n * 512:(n + 1) * 512],
                            start=(kc == 0), stop=(kc == 15)), reads=[b_m, b_won[n]],
                            writes=([PSB[pg0 + n]] if kc == 0 else []), pwrites=([] if kc == 0 else [PSB[pg0 + n]]))
                pbs = [PSB[pg0 + n] for n in range(4)]
                sm, b_sm = small.next()
                P.add("act", I("activation", out=junk, in_=pso, func=AF.Square, accum_out=sm[:, 0:1]),
                      reads=pbs, writes=[b_junk, b_sm])
                P.add("dve", I("tensor_scalar", out=sm[:, 1:2], in0=sm[:, 0:1], scalar1=1.0 / D, scalar2=EPS,
                                                              op0=ALU.mult, op1=ALU.add), reads=[b_sm], writes=[b_sm])
                P.add("act", I("activation", out=sm[:, 3:4], in_=sm[:, 1:2], func=AF.Ln), reads=[b_sm], writes=[b_sm])
                P.add("act", I("activation", out=sm[:, 2:3], in_=sm[:, 3:4], func=AF.Exp, scale=-0.5), reads=[b_sm], writes=[b_sm])
                ta, b_t = tmp.next()
                P.add("dve", I("scalar_tensor_tensor",
                    out=ta, in0=pso, scalar=sm[:, 2:3], in1=npost, op0=ALU.mult, op1=ALU.mult),
                    reads=pbs + [b_sm, b_np], writes=[b_t])
                ya, b_y = yb.next()
                P.add("pool", I("tensor_tensor", out=ya, in0=ta, in1=xa, op=ALU.add),
                      reads=[b_t, b_x], writes=[b_y])
                P.dma("pool", ydst[r0:r0 + 128, :], ya, b_y, reads=[b_y], pwrites=[b_ydst[T * 4 + i]])
        P.barrier()

    b_xin = [[Buf(f"xin{s}_{t}") for t in range(32)] for s in range(NS)]
    if "B" in phases:
        phase0()
    if "W" in phases:
        phaseW()
    for l in range(nlayer):
        for s in range(ns):
            if l == 0:
                xsrc, b_xsrc = x_in[s], b_xin[s]
            else:
                xsrc, b_xsrc = Y1[s], b_Y1[s]
            if l == nlayer - 1 and not cfg.get("y1only", False):
                ydst, b_ydst = y_out[s], b_Y[s]
            else:
                ydst, b_ydst = Y1[s], b_Y1[s]
            if "A" in phases:
                phaseA(l, s, xsrc, b_xsrc)
            if "B" in phases and not (cfg.get("skipB0") and s == 0):
                phaseB(l, s)
            if "C" in phases:
                phaseC(l, s, 1)
                phaseC(l, s, 0)
            if "D" in phases:
                phaseD(l, s, xsrc, b_xsrc, ydst, b_ydst)
    P.final_wait()
    with nc.Block() as block:
        P.emit(block)
    es.close()
    return nc, P


_CACHE = {}


def _in_maps(inputs, cfg=None):
    c = _consts()
    xp = np.asarray(inputs["x_prompt"], np.float32)
    xsamp = np.asarray(inputs["x_sample"], np.float32)
    maps = []
    for core in range(8):
        x = np.stack([xp[core], xsamp[core % 2]], axis=0)
        m = {"x": np.ascontiguousarray(x)}
        for k in ["rel_bias", "w_in", "w_gk_fwd", "w_gk_bwd", "b_gk_fwd", "b_gk_bwd", "sink", "gla_norm", "w_out",
                  "norm_pre", "norm_post"]:
            m[k] = np.ascontiguousarray(np.asarray(inputs[k], np.float32))
        m["c_ident"] = c["ident"]
        m["c_tri"] = c["tri"]
        m["c_oh"] = c["oh"]
        m["c_vm"] = c["vm"]
        m["c_ones_bf"] = c["ones_bf"]
        m["c_ones_f"] = c["ones_f"]
        m["c_J"] = c["J"]
        maps.append(m)
    return maps


def kernel(**inputs):
    if "nc" not in _CACHE:
        _CACHE["nc"] = build()[0]
    nc = _CACHE["nc"]
    maps = _in_maps(inputs)
    res = run_bass_kernel_spmd(nc, maps, core_ids=list(range(8)))
    ys = [r["y"] for r in res.results]
    y_prompt = np.stack([ys[c][0] for c in range(8)], axis=0).astype(np.float32)
    y_sample = np.stack([ys[c][1] for c in range(2)], axis=0).astype(np.float32)
    return (y_prompt, y_sample)
```

```python
import numpy as np
import ml_dtypes
import concourse.bass as bass
import concourse.mybir as mybir
from concourse.bass_utils import run_bass_kernel_spmd

F32 = mybir.dt.float32
BF16 = mybir.dt.bfloat16
AF = mybir.ActivationFunctionType
ALU = mybir.AluOpType

L = 4096
D = 2048
DIN = 5664
NS = 2
NLAYER = 2
EPS = 1e-6
ENGS = ["pe", "act", "dve", "pool", "sp"]
FOLD_WAIT = False
EIDX = {e: i for i, e in enumerate(ENGS)}


class Buf:
    __slots__ = ("name", "w", "r", "war", "sem", "cum", "uid")
    _n = [0]

    def __init__(self, name):
        Buf._n[0] += 1
        self.uid = Buf._n[0]
        self.name = name
        self.w = {}
        self.r = {}
        self.war = {}
        self.sem = None
        self.cum = 0


class Op:
    __slots__ = ("eng", "fn", "deps", "need_inc", "val", "sem", "is_dma", "order", "key", "seq", "clock")


class Prog:
    def __init__(self, nc, es):
        self.nc = nc
        self.es = es
        self.ops = {e: [] for e in ENGS}
        self.esem = {e: es.enter_context(nc.semaphore("sem_" + e)) for e in ENGS}
        self.order = 0
        self.last = {e: None for e in ENGS}
        self.pending_dma = {}
        self.nsem = len(ENGS)
        self.free_sems = []
        self.active_owners = []
        self.eclock = {e: [-1] * len(ENGS) for e in ENGS}
        self.eseq = {e: 0 for e in ENGS}

    def _mk(self, eng, fn, reads, writes, pwrites, owner, extra):
        op = Op()
        op.eng = eng
        op.fn = fn
        op.need_inc = False
        op.val = None
        op.sem = None
        op.is_dma = owner is not None
        self.order += 1
        op.order = self.order
        deps = {}

        ck = self.eclock[eng]

        def dep(o):
            if o is None:
                return
            if (not o.is_dma) and (not op.is_dma) and o.eng == "pe" and eng == "pe":
                return
            if (not o.is_dma) and ck[EIDX[o.eng]] >= o.seq:
                return
            cur = deps.get(o.key)
            if cur is None or o.order > cur.order:
                deps[o.key] = o

        for b in reads:
            for o in b.w.values():
                dep(o)
        for b in list(writes) + list(pwrites):
            if b.r:
                b.war = b.r
                b.r = {}
                b.w = {}
            for o in b.war.values():
                dep(o)
        for b in writes:
            for o in b.w.values():
                dep(o)
        for o in extra:
            dep(o)
        if op.is_dma:
            if owner.sem is None:
                if self.free_sems:
                    owner.sem, owner.cum = self.free_sems.pop()
                else:
                    owner.sem = self.es.enter_context(self.nc.semaphore("dsem%d" % self.nsem))
                    owner.cum = 0
                    self.nsem += 1
                self.active_owners.append(owner)
            owner.cum += 16
            op.sem = owner.sem
            op.val = owner.cum
            op.key = ("d", owner.uid)
            self.pending_dma[op.key] = op
        else:
            op.key = eng
        for o in deps.values():
            o.need_inc = True
            for i2 in range(len(ENGS)):
                if o.clock[i2] > ck[i2]:
                    ck[i2] = o.clock[i2]
            if (not o.is_dma) and o.seq > ck[EIDX[o.eng]]:
                ck[EIDX[o.eng]] = o.seq
        op.seq = self.eseq[eng]
        self.eseq[eng] += 1
        cl = list(ck)
        if not op.is_dma:
            cl[EIDX[eng]] = op.seq
        op.clock = tuple(cl)
        op.deps = list(deps.values())
        for b in reads:
            b.r[op.key] = op
        for b in writes:
            b.w = {op.key: op}
        for b in pwrites:
            b.w[op.key] = op
        self.ops[eng].append(op)
        self.last[eng] = op
        return op

    def add(self, eng, fn, reads=(), writes=(), pwrites=(), extra=()):
        return self._mk(eng, fn, reads, writes, pwrites, None, extra)

    def dma(self, eng, out, in_, owner, reads=(), writes=(), pwrites=(), **kw):
        return self._mk(eng, I("dma_start", out=out, in_=in_, **kw), reads, writes, pwrites, owner, ())

    def barrier(self):
        pend = list(self.pending_dma.values())
        self.pending_dma = {}
        n1 = []
        for e in ENGS:
            extra = list(pend)
            if self.last[e] is not None:
                extra.append(self.last[e])
            n1.append(self._mk(e, lambda en: en.nop(), (), (), (), None, extra))
        for e in ENGS:
            self._mk(e, lambda en: en.nop(), (), (), (), None, n1)
        for b in self.active_owners:
            self.free_sems.append((b.sem, b.cum))
            b.sem = None
        self.active_owners = []

    def final_wait(self):
        pend = list(self.pending_dma.values())
        self._mk("sp", lambda en: en.nop(), (), (), (), None, pend)
        self._mk("sp", lambda en: en.nop(), (), (), (), None, [self.last["sp"]])

    def emit(self, block):
        nc = self.nc
        counts = {}
        for e in ENGS:
            c = 0
            for op in self.ops[e]:
                if op.is_dma:
                    continue
                if op.need_inc:
                    c += 1
                    op.val = c
                    op.sem = self.esem[e]
            counts[e] = (len(self.ops[e]), c)
        self.counts = counts
        self.nwaits = {}

        def run(eng_name):
            def body(E):
                known = {}
                for op in self.ops[eng_name]:
                    need = []
                    for d in op.deps:
                        k = id(d.sem)
                        if known.get(k, 0) < d.val:
                            need.append((d.sem, d.val))
                            known[k] = d.val
                    fold = need.pop() if (need and FOLD_WAIT) else None
                    for (sm, vl) in need:
                        E.wait_ge(sm, vl)
                        self.nwaits[eng_name] = self.nwaits.get(eng_name, 0) + 1
                    ins = op.fn(E)
                    if fold is not None:
                        ins._wait_ge(fold[0], fold[1])
                    if op.is_dma:
                        ins.then_inc(op.sem, 16)
                    elif op.need_inc:
                        ins.then_inc(op.sem, 1)

            return body

        block.tensor(run("pe"))
        block.scalar(run("act"))
        block.vector(run("dve"))
        block.gpsimd(run("pool"))
        block.sync(run("sp"))


def I(name, *a, **k):
    def fn(e):
        return getattr(e, name)(*a, **k)
    return fn


class Arena:
    def __init__(self, ap, nwords):
        self.ap = ap
        self.n = nwords
        self.top = 0
        self.mark = 0

    def f32(self, n, name):
        n = (n + 7) // 8 * 8
        assert self.top + n <= self.n, ("arena overflow", name, self.top, n)
        a = self.ap[:, self.top:self.top + n]
        self.top += n
        return a, Buf(name)

    def bf16(self, n, name):
        w = (n + 1) // 2
        a, b = self.f32(w, name)
        return a.bitcast(BF16)[:, 0:n], b

    def set_mark(self):
        self.mark = self.top

    def reset(self):
        self.top = self.mark


class Rot:
    def __init__(self, items):
        self.items = items
        self.i = 0

    def next(self):
        it = self.items[self.i % len(self.items)]
        self.i += 1
        return it


def _consts():
    c = {}
    c["ident"] = np.eye(128, dtype=np.float32).astype(ml_dtypes.bfloat16)
    jj = np.arange(128)[:, None]
    ii = np.arange(128)[None, :]
    same = (jj // 64) == (ii // 64)
    g = -1.0 / 16.0
    tri = np.zeros((6, 128, 128), np.float32)
    tri[0] = np.where(same & (jj <= ii), g, 0.0)
    tri[1] = np.where(same & (jj >= ii), g, 0.0)
    tri[2] = np.where(same & (jj > ii), g, 0.0)
    tri[3] = np.where(same & (jj < ii), g, 0.0)
    tri[4] = np.where(same & (ii >= jj), 1.0, 0.0)
    tri[5] = np.where(same & (ii <= jj), 1.0, 0.0)
    c["tri"] = np.ascontiguousarray(tri.transpose(1, 0, 2))
    nb = 16
    max_exact = 8
    oh = np.zeros((32, 512), np.float32)
    vm = np.zeros((8, 512), np.float32)
    for idx in range(512):
        rel = 256 - idx
        n = abs(rel)
        if n > 128:
            continue
        if n < max_exact:
            v = n
        else:
            v = max_exact + int(np.log(max(n, 1) / max_exact) / np.log(128 / max_exact) * (nb - max_exact))
            v = min(v, nb - 1)
        b = (nb if rel > 0 else 0) + v
        oh[b, idx] = 1.0
        vm[:, idx] = 1.0
    c["oh"] = oh
    c["vm"] = vm
    c["ones_bf"] = np.ones((128, 128), np.float32).astype(ml_dtypes.bfloat16)
    c["ones_f"] = np.ones((128, 128), np.float32)
    c["J"] = np.ascontiguousarray(np.eye(128, dtype=np.float32)[::-1])
    return c


def _bucket_check():
    qi = np.arange(128)[:, None]
    kj = np.arange(384)[None, :]
    rel = kj - 128 - qi
    nb = 16
    max_exact = 8
    n = np.abs(rel)
    large = max_exact + (np.log(np.maximum(n, 1) / max_exact) / np.log(128 / max_exact) * (nb - max_exact)).astype(np.int32)
    large = np.minimum(large, nb - 1)
    bucket = (rel > 0).astype(np.int32) * nb + np.where(n < max_exact, n, large)
    return bucket, rel


def _blocks():
    blk = {j: {"fm": [], "tm": []} for j in range(12)}
    for c in range(4):
        blk[0]["fm"].append((c * 128, 128, "FB", c * 128, "cp"))
        blk[1]["fm"].append((c * 128, 128, "FB", 512 + c * 128, "cp"))
    blk[2]["fm"].append((0, 128, "FB", 1024, "cp"))
    blk[2]["fm"].append((128, 128, "FB", 1152, "cp"))
    blk[2]["tm"].append((256, 256, "TB", 0))
    for c in range(4):
        blk[3]["fm"].append((c * 128, 128, "FF", c * 128, "silu"))
        blk[4]["fm"].append((c * 128, 128, "FF", 512 + c * 128, "silu"))
        blk[5]["fm"].append((c * 128, 128, "FF", 1024 + c * 128, "cp"))
        blk[6]["fm"].append((c * 128, 128, "FF", 1536 + c * 128, "cp"))
        blk[9]["fm"].append((c * 128, 128, "FF", 2048 + c * 128, "silu"))
        blk[10]["fm"].append((c * 128, 128, "FF", 2560 + c * 128, "silu"))
    blk[6]["tm"].append((0, 512, "TF", 0))
    blk[7]["tm"].append((0, 512, "TB", 256))
    blk[8]["tm"].append((0, 512, "TB", 768))
    blk[11]["fm"].append((0, 32, "FF", 3072, "cp"))
    return blk


def build(cfg=None):
    cfg = dict(cfg or {})
    ns = cfg.get("ns", NS)
    nlayer = cfg.get("nlayer", NLAYER)
    phases = cfg.get("phases", "WABCD")
    ntA = cfg.get("ntA", 8)
    dbg = cfg.get("dbg", False)

    nc = bass.Bass("TRN2", target_bir_lowering=False)
    from contextlib import ExitStack
    es = ExitStack()
    es.enter_context(nc.allow_low_precision("bf16 matmul operands, fp32 accumulation"))

    def din(name, shape, dt=F32):
        return nc.dram_tensor(name, list(shape), dt, kind="ExternalInput").ap()

    def dscr(name, shape, dt, out=False):
        return nc.dram_tensor(name, list(shape), dt, kind=("ExternalOutput" if out else "Internal")).ap()

    x_in = din("x", [NS, L, D])
    rel_bias = din("rel_bias", [32, 8])
    w_in = din("w_in", [NLAYER, D, DIN])
    w_gk = [din("w_gk_fwd", [NLAYER, 16, 512]), din("w_gk_bwd", [NLAYER, 16, 512])]
    b_gk = [din("b_gk_fwd", [NLAYER, 512]), din("b_gk_bwd", [NLAYER, 512])]
    sink = din("sink", [NLAYER, 8])
    gla_norm = din("gla_norm", [NLAYER, 256])
    w_out = din("w_out", [NLAYER, D, D])
    norm_pre = din("norm_pre", [NLAYER, D])
    norm_post = din("norm_post", [NLAYER, D])
    c_ident = din("c_ident", [128, 128], BF16)
    c_tri = din("c_tri", [128, 6, 128])
    c_oh = din("c_oh", [32, 512])
    c_vm = din("c_vm", [8, 512])
    c_ones_bf = din("c_ones_bf", [128, 128], BF16)
    c_ones_f = din("c_ones_f", [128, 128])
    c_J = din("c_J", [128, 128])

    y_out = dscr("y", [NS, L, D], F32, out=True)
    so = dbg
    wbf = dscr("wbf", [NLAYER, D, DIN], BF16, out=False)
    woutbf = dscr("woutbf", [NLAYER, D, D], BF16, out=False)
    def shared(name, shape, dt):
        t = dscr(name + "0", shape, dt, out=so)
        return [t for _ in range(NS)]

    FB = shared("FB", [1280, L], BF16)
    FF = shared("FF", [3104, L], F32)
    TB = shared("TB", [L, 1280], BF16)
    TF = shared("TF", [L, 512], F32)
    MIX = shared("MIX", [D, L], BF16)
    OB = shared("OB", [1024, L], F32)
    if dbg:
        Y1 = [dscr(f"Y1{s}", [L, D], F32, out=so) for s in range(NS)]
    else:
        Y1 = [y_out[s] for s in range(NS)]
    UB = dscr("UB", [8, 512], F32, out=False)

    NW = 52000
    arena_t = es.enter_context(nc.sbuf_tensor("arena", [128, NW], F32))
    psum_t = es.enter_context(nc.psum_tensor("psum", [128, 4096], F32))
    A = Arena(arena_t, NW)
    P = Prog(nc, es)

    def bank(i, n=1):
        return psum_t[:, i * 512:(i + n) * 512]

    PSB = [Buf(f"psb{i}") for i in range(8)]

    b_wbf = [Buf(f"wbf{l}") for l in range(NLAYER)]
    b_wout = [Buf(f"woutbf{l}") for l in range(NLAYER)]
    def sharedb(name):
        b = Buf(name)
        return [b for _ in range(NS)]

    b_FB = sharedb("bFB")
    b_FF = sharedb("bFF")
    b_TB = sharedb("bTB")
    b_TF = sharedb("bTF")
    b_MIX = sharedb("bMIX")
    b_OB = sharedb("bOB")
    b_Y = [[Buf(f"bY{s}_{t}") for t in range(32)] for s in range(NS)]
    b_Y1 = [[Buf(f"bY1{s}_{t}") for t in range(32)] for s in range(NS)] if dbg else b_Y
    b_UB = Buf("bUB")

    ident, b_ident = A.bf16(128, "ident")
    tri, b_tri = A.f32(6 * 128, "tri")
    tri3 = tri.rearrange("p (a b) -> p a b", a=6)
    ones_bf, b_ones_bf = A.bf16(128, "ones_bf")
    ones_f, b_ones_f = A.f32(128, "ones_f")
    expbT, b_expbT = A.f32(3 * 8 * 128, "expbT")
    expbT4 = expbT.rearrange("p (o h q) -> p o h q", o=3, h=8)
    esink, b_esink = A.f32(8 * NLAYER, "esink")
    gnorm, b_gnorm = A.f32(2 * NLAYER, "gnorm")
    P.dma("sp", ident, c_ident, b_ident, writes=[b_ident])
    P.dma("sp", tri3, c_tri, b_tri, writes=[b_tri])
    P.dma("sp", ones_bf, c_ones_bf, b_ones_bf, writes=[b_ones_bf])
    P.dma("sp", ones_f, c_ones_f, b_ones_f, writes=[b_ones_f])
    for l in range(NLAYER):
        P.dma("sp", esink[:, l * 8:(l + 1) * 8], sink[l].partition_broadcast(128), b_esink, pwrites=[b_esink])
        for v in range(2):
            P.dma("sp", gnorm[:, l * 2 + v:l * 2 + v + 1],
                  gla_norm[l, v * 128:(v + 1) * 128].rearrange("(p o) -> p o", o=1), b_gnorm, pwrites=[b_gnorm])
    P.add("act", I("activation", out=esink, in_=esink, func=AF.Exp), reads=[b_esink], writes=[b_esink])
    A.set_mark()

    def phase0():
        A.reset()
        oh, b_oh = A.f32(512, "oh")
        vm, b_vm = A.f32(512, "vm")
        rb, b_rb = A.f32(8, "rb")
        u, b_u = A.f32(512, "u")
        P.dma("sp", oh[0:32, :], c_oh, b_oh, writes=[b_oh])
        P.dma("sp", vm[0:8, :], c_vm, b_vm, writes=[b_vm])
        P.dma("sp", rb[0:32, :], rel_bias, b_rb, writes=[b_rb])
        ps = bank(0)
        P.add("pe", I("matmul", ps[0:8, :], lhsT=rb[0:32, 0:8], rhs=oh[0:32, :], start=True, stop=True),
              reads=[b_rb, b_oh], writes=[PSB[0]])
        P.add("act", I("activation", out=u[0:8, :], in_=ps[0:8, :], func=AF.Exp), reads=[PSB[0]], writes=[b_u])
        P.add("dve", I("tensor_tensor", out=u[0:8, :], in0=u[0:8, :], in1=vm[0:8, :], op=ALU.mult),
              reads=[b_u, b_vm], writes=[b_u])
        P.dma("sp", UB, u[0:8, :], b_u, reads=[b_u], writes=[b_UB])
        ubt = UB.tensor
        jm, b_jm = A.f32(128, "jm")
        P.dma("sp", jm, c_J, b_jm, writes=[b_jm])
        for o in range(3):
            wt, b_wt = A.f32(1024, f"wt{o}")
            for hh in range(2):
                src_ap = bass.AP(tensor=ubt, offset=129 - (o - 1) * 128 + hh * 4 * 512, ap=[[1, 128], [512, 4], [1, 128]])
                P.dma("sp", wt.rearrange("p (h q) -> p h q", h=8)[:, hh * 4:(hh + 1) * 4, :], src_ap, b_wt, reads=[b_UB], pwrites=[b_wt])
            for hf in range(2):
                pj = bank(1 + hf)
                P.add("pe", I("matmul", pj, lhsT=jm, rhs=wt[:, hf * 512:(hf + 1) * 512], start=True, stop=True),
                      reads=[b_jm, b_wt], writes=[PSB[1 + hf]])
                P.add("dve", I("tensor_copy", out=expbT4[:, o, hf * 4:(hf + 1) * 4, :],
                               in_=pj.rearrange("p (h q) -> p h q", h=4)), reads=[PSB[1 + hf]], pwrites=[b_expbT])
        P.barrier()

    def phaseW():
        lanes = [Buf(f"lane{i}") for i in range(4)]
        k = 0
        for l in range(nlayer):
            for r in range(8):
                src = w_in[l, r * 256:(r + 1) * 256, :].rearrange("r (a b) -> r a b", a=3)
                dst = wbf[l, r * 256:(r + 1) * 256, :].rearrange("r (a b) -> r a b", a=3)
                P.dma("pool", dst, src, lanes[k % 4], writes=[lanes[k % 4]], pwrites=[b_wbf[l]])
                k += 1
            for r in range(8):
                src = w_out[l, r * 256:(r + 1) * 256, :]
                dst = woutbf[l, r * 256:(r + 1) * 256, :]
                P.dma("pool", dst, src, lanes[k % 4], writes=[lanes[k % 4]], pwrites=[b_wout[l]])
                k += 1

    BLK = _blocks()

    def phaseA(l, s, xsrc, b_xsrc):
        A.reset()
        npre, b_npre = A.f32(D, "npre")
        P.dma("sp", npre, norm_pre[l].partition_broadcast(128), b_npre, writes=[b_npre])
        xs = Rot([A.f32(D, f"xs{i}") for i in range(2)])
        junk, b_junk = A.bf16(D, "junk")
        hb = Rot([A.bf16(D, f"hb{i}") for i in range(2)])
        hT = Rot([A.bf16(16 * 512, f"hT{i}") for i in range(2)])
        wb = Rot([A.bf16(16 * 512, f"wb{i}") for i in range(3)])
        stg = Rot([A.f32(512, f"stg{i}") for i in range(6)])
        small = Rot([A.f32(8, f"sm{i}") for i in range(4)])
        pt = psum_t[:, 0:1024].bitcast(BF16)
        po = Rot([2, 3, 4, 5, 6, 7])
        dst_ap = {"FB": FB[s], "FF": FF[s], "TB": TB[s], "TF": TF[s]}
        dst_buf = {"FB": b_FB[s], "FF": b_FF[s], "TB": b_TB[s], "TF": b_TF[s]}
        ncopy = [0]

        def evac(psap, mode, dt_bf, shape_p, n):
            sa, sb = stg.next()
            if dt_bf:
                o = sa.bitcast(BF16)[0:shape_p, 0:n]
            else:
                o = sa[0:shape_p, 0:n]
            return o, sb

        hT_of = {}

        def front_ew(T, i):
            if T not in hT_of:
                hTa, b_hT = hT.next()
                hT_of[T] = (hTa.rearrange("p (k t) -> p k t", k=16), b_hT)
            xa, b_x = xs.next()
            r0 = (T * 4 + i) * 128
            P.dma("sp", xa, xsrc[r0:r0 + 128, :], b_x, reads=[b_xsrc[T * 4 + i]], writes=[b_x])
            sm, b_sm = small.next()
            P.add("act", I("activation", out=junk, in_=xa, func=AF.Square, accum_out=sm[:, 0:1]),
                  reads=[b_x], writes=[b_junk, b_sm])
            P.add("dve", I("tensor_scalar", out=sm[:, 1:2], in0=sm[:, 0:1], scalar1=1.0 / D, scalar2=EPS,
                           op0=ALU.mult, op1=ALU.add), reads=[b_sm], writes=[b_sm])
            P.add("act", I("activation", out=sm[:, 3:4], in_=sm[:, 1:2], func=AF.Ln), reads=[b_sm], writes=[b_sm])
            P.add("act", I("activation", out=sm[:, 2:3], in_=sm[:, 3:4], func=AF.Exp, scale=-0.5), reads=[b_sm], writes=[b_sm])
            ha, b_h = hb.next()
            P.add("dve", I("scalar_tensor_tensor", out=ha, in0=xa, scalar=sm[:, 2:3], in1=npre, op0=ALU.mult, op1=ALU.mult),
                  reads=[b_x, b_sm, b_npre], writes=[b_h])
            return (T, i, ha, b_h)

        def front_pe(ctx):
            T, i, ha, b_h = ctx
            hT3, b_hT = hT_of[T]
            for kc in range(16):
                P.add("pe", I("transpose", pt[:, kc * 128:(kc + 1) * 128], ha[:, kc * 128:(kc + 1) * 128], ident),
                      reads=[b_h, b_ident], pwrites=[PSB[kc // 8]])
            for hf in range(2):
                src_ = pt[:, hf * 1024:(hf + 1) * 1024].rearrange("p (k t) -> p k t", k=8)
                dstv = hT3[:, hf * 8:(hf + 1) * 8, i * 128:(i + 1) * 128]
                if hf == 0:
                    P.add("act", I("copy", out=dstv, in_=src_), reads=[PSB[hf]], pwrites=[b_hT])
                else:
                    P.add("dve", I("tensor_copy", out=dstv, in_=src_), reads=[PSB[hf]], pwrites=[b_hT])

        def groups(T):
            hT3, b_hT = hT_of[T]
            for j in range(12):
                wa, b_w = wb.next()
                w3 = wa.rearrange("p (k c) -> p k c", k=16)
                ncol = 512 if j < 11 else 32
                for hk in range(4):
                    P.dma("sp", w3[:, hk * 4:(hk + 1) * 4, 0:ncol],
                          wbf[l, hk * 512:(hk + 1) * 512, j * 512:j * 512 + ncol].rearrange("(k p) c -> p k c", p=128),
                          b_w, reads=[b_wbf[l]], pwrites=[b_w])
                for (c0, ncl, dname, drow, mode) in BLK[j]["fm"]:
                    pb = po.next()
                    ps = bank(pb)
                    for kc in range(16):
                        P.add("pe", I("matmul", ps[0:ncl, :], lhsT=w3[:, kc, c0:c0 + ncl], rhs=hT3[:, kc, :], start=(kc == 0), stop=(kc == 15)),
                            reads=[b_w, b_hT], writes=([PSB[pb]] if kc == 0 else []), pwrites=([] if kc == 0 else [PSB[pb]]))
                    isbf = (dname == "FB")
                    o, b_o = evac(ps, mode, isbf, ncl, 512)
                    if mode == "silu":
                        P.add("act", I("activation", out=o, in_=ps[0:ncl, :], func=AF.Silu), reads=[PSB[pb]], writes=[b_o])
                    else:
                        ncopy[0] += 1
                        if ncopy[0] % 3 == 0:
                            P.add("act", I("copy", out=o, in_=ps[0:ncl, :]), reads=[PSB[pb]], writes=[b_o])
                        else:
                            P.add("dve", I("tensor_copy", out=o, in_=ps[0:ncl, :]), reads=[PSB[pb]], writes=[b_o])
                    P.dma("pool", dst_ap[dname][drow:drow + ncl, T * 512:(T + 1) * 512], o, b_o, reads=[b_o], pwrites=[dst_buf[dname]])
                    yield
                for (c0, ncl, dname, dcol) in BLK[j]["tm"]:
                    for i in range(4):
                        pb = po.next()
                        ps = bank(pb)
                        for kc in range(16):
                            P.add("pe", I("matmul", ps[:, 0:ncl], lhsT=hT3[:, kc, i * 128:(i + 1) * 128], rhs=w3[:, kc, c0:c0 + ncl],
                                start=(kc == 0), stop=(kc == 15)),
                                reads=[b_w, b_hT], writes=([PSB[pb]] if kc == 0 else []), pwrites=([] if kc == 0 else [PSB[pb]]))
                        isbf = (dname == "TB")
                        o, b_o = evac(ps, "cp", isbf, 128, ncl)
                        ncopy[0] += 1
                        if ncopy[0] % 3 == 0:
                            P.add("act", I("copy", out=o, in_=ps[:, 0:ncl]), reads=[PSB[pb]], writes=[b_o])
                        else:
                            P.add("dve", I("tensor_copy", out=o, in_=ps[:, 0:ncl]), reads=[PSB[pb]], writes=[b_o])
                        r0 = (T * 4 + i) * 128
                        P.dma("pool", dst_ap[dname][r0:r0 + 128, dcol:dcol + ncl], o, b_o, reads=[b_o], pwrites=[dst_buf[dname]])
                        yield

        for i in range(4):
            front_pe(front_ew(0, i))
        for T in range(ntA):
            pend = {}
            g = 0
            for _ in groups(T):
                g += 1
                if T + 1 < ntA:
                    for i in range(4):
                        if g == 4 + 11 * i:
                            pend[i] = front_ew(T + 1, i)
                        if g == 9 + 11 * i:
                            front_pe(pend.pop(i))
            assert not pend
        P.barrier()

    def phaseB(l, s):
        A.reset()
        kT, b_kT = A.bf16(2 * L, "kT")
        kT3 = kT.rearrange("p (g t) -> p g t", g=2)
        V, b_V = A.bf16(32 * 256, "V")
        V3 = V.rearrange("p (b c) -> p b c", b=32)
        P.dma("sp", kT3, FB[s][1024:1280, :].rearrange("(g p) t -> p g t", p=128), b_kT, reads=[b_FB[s]], writes=[b_kT])
        for part in range(8):
            P.dma("sp", V3[:, part * 4:(part + 1) * 4, :],
                  TB[s][part * 512:(part + 1) * 512, 0:256].rearrange("(b p) c -> p b c", p=128), b_V,
                  reads=[b_TB[s]], pwrites=[b_V])
        qT = Rot([A.bf16(8 * 512, f"qT{i}") for i in range(2)])
        sza = Rot([A.f32(8 * 512, f"sza{i}") for i in range(2)])
        atb = Rot([A.bf16(8 * 512, f"atb{i}") for i in range(2)])
        ef = Rot([A.f32(512, f"ef{i}") for i in range(4)])
        pT = Rot([A.bf16(512, f"pT{i}") for i in range(6)])
        rec = Rot([A.f32(512, f"rec{i}") for i in range(2)])
        at = Rot([A.f32(512, f"at{i}") for i in range(2)])
        stb = Rot([0, 1, 2, 3])
        otb = Rot([4, 5])
        dnb = Rot([6, 7])
        scale = 128 ** -0.5
        es_l = esink[:, l * 8:(l + 1) * 8]
        esrow, b_esrow = A.bf16(1024, "esrow")
        P.add("dve", I("tensor_copy", out=esrow[0:1, :].rearrange("p (h q) -> p h q", h=8),
                       in_=es_l[0:1, :].unsqueeze(2).to_broadcast([1, 8, 128])), reads=[b_esink], writes=[b_esrow])
        for T in range(8):
            qa, b_q = qT.next()
            q3 = qa.rearrange("p (h t) -> p h t", h=8)
            za, b_z = sza.next()
            z3 = za.rearrange("p (h t) -> p h t", h=8)
            aa, b_a = atb.next()
            a3 = aa.rearrange("p (h t) -> p h t", h=8)
            for hh in range(2):
                P.dma("sp", q3[:, hh * 4:(hh + 1) * 4, :],
                      FB[s][hh * 512:(hh + 1) * 512, T * 512:(T + 1) * 512].rearrange("(h p) t -> p h t", p=128), b_q,
                      reads=[b_FB[s]], pwrites=[b_q])
                P.dma("sp", z3[:, hh * 4:(hh + 1) * 4, :],
                      FF[s][hh * 512:(hh + 1) * 512, T * 512:(T + 1) * 512].rearrange("(h p) t -> p h t", p=128), b_z,
                      reads=[b_FF[s]], pwrites=[b_z])
            units = [(ib, g) for ib in range(4) for g in range(2)]
            pend = None

            def front(ib, g):
                qi = T * 4 + ib
                offs = [o for o in (-1, 0, 1) if 0 <= qi + o < 32]
                pts = []
                for o in offs:
                    kb = qi + o
                    sbk = stb.next()
                    st = bank(sbk)
                    P.add("pe", I("matmul",
                        st, lhsT=kT3[:, g, kb * 128:(kb + 1) * 128], rhs=q3[:, 4 * g:4 * g + 4, ib * 128:(ib + 1) * 128],
                        start=True, stop=True), reads=[b_kT, b_q], writes=[PSB[sbk]])
                    ea, b_e = ef.next()
                    P.add("act", I("activation", out=ea, in_=st, func=AF.Exp, scale=scale),
                          reads=[PSB[sbk]], writes=[b_e])
                    pa, b_p = pT.next()
                    P.add(("dve" if o == 0 else "pool"), I("tensor_tensor",
                        out=pa.rearrange("p (h q) -> p h q", h=4), in0=ea.rearrange("p (h q) -> p h q", h=4),
                        in1=expbT4[:, o + 1, 4 * g:4 * g + 4, :], op=ALU.mult), reads=[b_e, b_expbT], writes=[b_p])
                    pts.append((kb, pa, b_p))
                return (ib, g, pts)

            def back(u):
                ib, g, pts = u
                ob_ = otb.next()
                db_ = dnb.next()
                oT = bank(ob_)
                dn = bank(db_)
                n = len(pts)
                for k, (kb, pa, b_p) in enumerate(pts):
                    P.add("pe", I("matmul", oT, lhsT=V3[:, kb, g * 128:(g + 1) * 128], rhs=pa, start=(k == 0), stop=(k == n - 1)),
                        reads=[b_V, b_p], writes=([PSB[ob_]] if k == 0 else []), pwrites=([] if k == 0 else [PSB[ob_]]))
                P.add("pe", I("matmul", dn, lhsT=ones_bf[0:1, :], rhs=esrow[0:1, g * 512:(g + 1) * 512], start=True, stop=False),
                      reads=[b_ones_bf, b_esrow], writes=[PSB[db_]])
                for k, (kb, pa, b_p) in enumerate(pts):
                    P.add("pe", I("matmul", dn, lhsT=ones_bf, rhs=pa, start=False, stop=(k == n - 1)),
                        reads=[b_ones_bf, b_p], pwrites=[PSB[db_]])
                ra, b_r = rec.next()
                P.add("dve", I("reciprocal", out=ra, in_=dn), reads=[PSB[db_]], writes=[b_r])
                ta, b_t = at.next()
                P.add("pool", I("tensor_tensor", out=ta.rearrange("p (h q) -> p h q", h=4), in0=ra.rearrange("p (h q) -> p h q", h=4),
                    in1=z3[:, 4 * g:4 * g + 4, ib * 128:(ib + 1) * 128], op=ALU.mult), reads=[b_r, b_z], writes=[b_t])
                P.add("dve", I("tensor_tensor", out=a3[:, 4 * g:4 * g + 4, ib * 128:(ib + 1) * 128],
                    in0=oT.rearrange("p (h q) -> p h q", h=4), in1=ta.rearrange("p (h q) -> p h q", h=4), op=ALU.mult),
                    reads=[PSB[ob_], b_t], pwrites=[b_a])

            for (ib, g) in units:
                u = front(ib, g)
                if pend is not None:
                    back(pend)
                pend = u
            back(pend)
            for hh in range(2):
                P.dma("pool", MIX[s][hh * 512:(hh + 1) * 512, T * 512:(T + 1) * 512].rearrange("(h p) t -> p h t", p=128),
                      a3[:, hh * 4:(hh + 1) * 4, :], b_a, reads=[b_a], pwrites=[b_MIX[s]])
        P.barrier()

    def phaseC(l, s, d):
        A.reset()
        wg, b_wg = A.f32(512, "wg")
        P.dma("sp", wg[0:16, :], w_gk[d][l], b_wg, pwrites=[b_wg])
        P.dma("sp", wg[16:17, :], b_gk[d][l:l + 1, :], b_wg, pwrites=[b_wg])
        lr = Rot([A.f32(512, f"lr{i}") for i in range(2)])
        for (a_, b_) in lr.items:
            P.add("pool", I("memset", a_[0:17, :], 1.0), writes=[b_])
        Sst, _bS = A.f32(1024, "S")
        S3 = Sst.rearrange("p (h v) -> p h v", h=4)
        b_Sh = [Buf(f"S{h}") for h in range(4)]
        P.add("pool", I("memset", Sst, 0.0), writes=b_Sh)
        Sb = Rot([A.bf16(1024, f"Sb{i}") for i in range(2)])
        sb_cur, b_sbcur = Sb.next()
        P.add("pool", I("memset", sb_cur, 0.0), writes=[b_sbcur])
        qbT = Rot([A.f32(4 * 512, f"qbT{i}") for i in range(2)])
        kbT = Rot([A.f32(4 * 512, f"kbT{i}") for i in range(2)])
        kbk = Rot([A.f32(4 * 512, f"kbk{i}") for i in range(2)])
        vbk = Rot([A.bf16(4 * 1024, f"vbk{i}") for i in range(2)])
        if d == 0:
            obl = Rot([A.f32(8 * 128, f"obl{i}") for i in range(2)])
            szb = Rot([A.f32(8 * 128, f"szb{i}") for i in range(2)])
            mixb = Rot([A.bf16(8 * 512, f"mixb{i}") for i in range(2)])
        else:
            obs = Rot([A.f32(8 * 512, f"obs{i}") for i in range(2)])
        e1 = Rot([A.f32(512, f"e1{i}") for i in range(2)])
        spb = Rot([A.f32(512, f"sp{i}") for i in range(2)])
        EbT = Rot([A.f32(512, f"EbT{i}") for i in range(2)])
        EnbT = Rot([A.f32(512, f"EnbT{i}") for i in range(2)])
        Etl = Rot([A.f32(512, f"Etl{i}") for i in range(2)])
        qd = Rot([A.bf16(512, f"qd{i}") for i in range(2)])
        kd = Rot([A.bf16(512, f"kd{i}") for i in range(2)])
        ktl = Rot([A.bf16(512, f"ktl{i}") for i in range(2)])
        ATb = Rot([A.bf16(512, f"ATb{i}") for i in range(2)])
        of = Rot([A.f32(1024, f"of{i}") for i in range(2)])
        sq = Rot([A.bf16(1024, f"sq{i}") for i in range(2)])
        rs = Rot([A.f32(512, f"rs{i}") for i in range(2)])
        TRI = tri3[:, (1 if d else 0), :]
        TT = tri3[:, (3 if d else 2), :]
        MSK = tri3[:, (5 if d else 4), :]
        P_GATE, P_BT, P_TAIL, P_AT, P_OT, P_KV = 0, 1, 2, 3, 4, 6
        gl = gnorm[:, l * 2:(l + 1) * 2]
        dk_scale = 128 ** -0.5
        torder = list(range(8)) if d == 0 else list(range(7, -1, -1))
        for T in torder:
            lra, b_lr = lr.next()
            P.dma("sp", lra[0:16, :], FF[s][3072 + 16 * d:3072 + 16 * d + 16, T * 512:(T + 1) * 512], b_lr,
                  reads=[b_FF[s]], pwrites=[b_lr])
            qa, b_q = qbT.next()
            q3 = qa.rearrange("p (h t) -> p h t", h=4)
            P.dma("sp", q3, FF[s][1024:1536, T * 512:(T + 1) * 512].rearrange("(h p) t -> p h t", p=128), b_q,
                  reads=[b_FF[s]], writes=[b_q])
            ka, b_k = kbT.next()
            k3 = ka.rearrange("p (h t) -> p h t", h=4)
            P.dma("sp", k3, FF[s][1536:2048, T * 512:(T + 1) * 512].rearrange("(h p) t -> p h t", p=128), b_k,
                  reads=[b_FF[s]], writes=[b_k])
            kka, b_kk = kbk.next()
            kk3 = kka.rearrange("p (i c) -> p i c", i=4)
            P.dma("sp", kk3, TF[s][T * 512:(T + 1) * 512, :].rearrange("(i p) c -> p i c", p=128), b_kk,
                  reads=[b_TF[s]], writes=[b_kk])
            va, b_v = vbk.next()
            v3 = va.rearrange("p (i c) -> p i c", i=4)
            P.dma("sp", v3, TB[s][T * 512:(T + 1) * 512, 256:1280].rearrange("(i p) c -> p i c", p=128), b_v,
                  reads=[b_TB[s]], writes=[b_v])
            if d == 0:
                mxa, b_mx = mixb.next()
                mx3 = mxa.rearrange("p (c t) -> p c t", c=8)
            else:
                osa, b_os = obs.next()
                os3 = osa.rearrange("p (c t) -> p c t", c=8)
            iorder = list(range(4)) if d == 0 else list(range(3, -1, -1))
            for i in iorder:
                tk = slice(i * 128, (i + 1) * 128)
                if d == 0:
                    t0 = T * 512 + i * 128
                    oba, b_ob = obl.next()
                    ob3 = oba.rearrange("p (c t) -> p c t", c=8)
                    for hh in range(2):
                        P.dma("sp", ob3[:, hh * 4:(hh + 1) * 4, :],
                              OB[s][hh * 512:(hh + 1) * 512, t0:t0 + 128].rearrange("(c p) t -> p c t", p=128), b_ob,
                              reads=[b_OB[s]], pwrites=[b_ob])
                    sza_, b_sz = szb.next()
                    sz3 = sza_.rearrange("p (c t) -> p c t", c=8)
                    for hh in range(2):
                        P.dma("sp", sz3[:, hh * 4:(hh + 1) * 4, :],
                              FF[s][2048 + hh * 512:2048 + (hh + 1) * 512, t0:t0 + 128].rearrange("(c p) t -> p c t", p=128), b_sz,
                              reads=[b_FF[s]], pwrites=[b_sz])
                pg = bank(P_GATE)
                P.add("pe", I("matmul", pg, lhsT=lra[0:17, tk], rhs=wg[0:17, :], start=True, stop=True),
                      reads=[b_lr, b_wg], writes=[PSB[P_GATE]])
                e1a, b_e1 = e1.next()
                P.add("act", I("activation", out=e1a, in_=pg, func=AF.Exp, scale=-1.0),
                      reads=[PSB[P_GATE]], writes=[b_e1])
                spa, b_sp = spb.next()
                P.add("act", I("activation", out=spa, in_=e1a, func=AF.Ln, bias=1.0),
                      reads=[b_e1], writes=[b_sp])
                pbt = bank(P_BT)
                for h in range(4):
                    P.add("pe", I("matmul",
                        pbt[:, h * 128:(h + 1) * 128], lhsT=spa[:, h * 128:(h + 1) * 128], rhs=TRI, start=True, stop=True),
                        reads=[b_sp, b_tri], writes=([PSB[P_BT]] if h == 0 else []), pwrites=([] if h == 0 else [PSB[P_BT]]))
                ptl = bank(P_TAIL)
                P.add("pe", I("matmul", ptl, lhsT=TT, rhs=spa, start=True, stop=True),
                      reads=[b_sp, b_tri], writes=[PSB[P_TAIL]])
                eb, b_eb = EbT.next()
                enb, b_enb = EnbT.next()
                etl, b_etl = Etl.next()
                P.add("act", I("activation", out=eb, in_=pbt, func=AF.Exp), reads=[PSB[P_BT]], writes=[b_eb])
                P.add("act", I("activation", out=enb, in_=pbt, func=AF.Exp, scale=-1.0),
                      reads=[PSB[P_BT]], writes=[b_enb])
                P.add("act", I("activation", out=etl, in_=ptl, func=AF.Exp), reads=[PSB[P_TAIL]], writes=[b_etl])
                eb3 = eb.rearrange("p (h t) -> p h t", h=4)
                qda, b_qd = qd.next()
                qd3 = qda.rearrange("p (h t) -> p h t", h=4)
                P.add("dve", I("scalar_tensor_tensor",
                    out=qd3, in0=q3[:, :, tk], scalar=dk_scale, in1=eb3, op0=ALU.mult, op1=ALU.mult),
                    reads=[b_q, b_eb], writes=[b_qd])
                kda, b_kd = kd.next()
                kd3 = kda.rearrange("p (h t) -> p h t", h=4)
                P.add("pool", I("tensor_tensor",
                    out=kd3, in0=k3[:, :, tk], in1=enb.rearrange("p (h t) -> p h t", h=4), op=ALU.mult),
                    reads=[b_k, b_enb], writes=[b_kd])
                kta, b_kt = ktl.next()
                P.add("pool", I("tensor_tensor",
                    out=kta, in0=kk3[:, i, :], in1=etl, op=ALU.mult), reads=[b_kk, b_etl], writes=[b_kt])
                pat = bank(P_AT)
                for h in range(4):
                    P.add("pe", I("matmul",
                        pat[:, h * 128:(h + 1) * 128], lhsT=kd3[:, h, :], rhs=qd3[:, h, :], start=True, stop=True),
                        reads=[b_kd, b_qd], writes=([PSB[P_AT]] if h == 0 else []), pwrites=([] if h == 0 else [PSB[P_AT]]))
                aba, b_ab = ATb.next()
                ab3 = aba.rearrange("p (h t) -> p h t", h=4)
                P.add("dve", I("tensor_tensor",
                    out=ab3, in0=pat.rearrange("p (h t) -> p h t", h=4), in1=MSK.unsqueeze(1).to_broadcast([128, 4, 128]),
                    op=ALU.mult), reads=[PSB[P_AT], b_tri], writes=[b_ab])
                pot = bank(P_OT, 2)
                corder = [0, 1] if d == 0 else [1, 0]
                first_in_bank = {0: True, 1: True}
                for h in range(4):
                    for v in range(2):
                        c8 = h * 2 + v
                        bk = c8 // 4
                        P.add("pe", I("matmul",
                            pot[:, c8 * 128:(c8 + 1) * 128], lhsT=v3[:, i, h * 256 + v * 128:h * 256 + (v + 1) * 128],
                            rhs=ab3[:, h, :], start=first_in_bank[bk], stop=False, skip_group_check=True),
                            reads=[b_v, b_ab], writes=([PSB[P_OT + bk]] if first_in_bank[bk] else []),
                            pwrites=([] if first_in_bank[bk] else [PSB[P_OT + bk]]))
                        first_in_bank[bk] = False
                for ci, c in enumerate(corder):
                    cs = slice(c * 64, (c + 1) * 64)
                    sbv = sb_cur.rearrange("p (h v) -> p h v", h=4)
                    for h in range(4):
                        for v in range(2):
                            c8 = h * 2 + v
                            bk = c8 // 4
                            P.add("pe", I("matmul",
                                pot[:, c8 * 128 + c * 64:c8 * 128 + (c + 1) * 64], lhsT=sbv[:, h, v * 128:(v + 1) * 128],
                                rhs=qd3[:, h, c * 64:(c + 1) * 64], start=False, stop=(ci == 1), skip_group_check=True),
                                reads=[b_sbcur, b_qd], pwrites=[PSB[P_OT + bk]])
                    pkv = bank(P_KV, 2)
                    for h in range(4):
                        bk = h // 2
                        P.add("pe", I("matmul",
                            pkv[:, h * 256:(h + 1) * 256], lhsT=kta[cs, h * 128:(h + 1) * 128], rhs=v3[cs, i, h * 256:(h + 1) * 256],
                            start=True, stop=True), reads=[b_kt, b_v],
                            writes=([PSB[P_KV + bk]] if h % 2 == 0 else []), pwrites=([] if h % 2 == 0 else [PSB[P_KV + bk]]))
                    if d == 0:
                        dcol = 63 if c == 0 else 127
                    else:
                        dcol = 64 if c == 1 else 0
                    for h in range(4):
                        P.add("dve", I("scalar_tensor_tensor",
                            out=S3[:, h, :], in0=S3[:, h, :], scalar=eb3[:, h, dcol:dcol + 1], in1=pkv[:, h * 256:(h + 1) * 256],
                            op0=ALU.mult, op1=ALU.add), reads=[b_Sh[h], b_eb, PSB[P_KV + h // 2]], writes=[b_Sh[h]])
                    sb_cur, b_sbcur = Sb.next()
                    P.add("act", I("copy", out=sb_cur, in_=Sst), reads=b_Sh, writes=[b_sbcur])
                if d == 1:
                    P.add("act", I("copy", out=os3[:, :, tk], in_=pot.rearrange("p (c t) -> p c t", c=8)),
                          reads=[PSB[P_OT], PSB[P_OT + 1]], pwrites=[b_os])
                else:
                    ofa, b_of = of.next()
                    of3 = ofa.rearrange("p (c t) -> p c t", c=8)
                    P.add("dve", I("tensor_tensor",
                        out=of3, in0=pot.rearrange("p (c t) -> p c t", c=8), in1=ob3, op=ALU.add),
                        reads=[PSB[P_OT], PSB[P_OT + 1], b_ob], writes=[b_of])
                    sqa, b_sq = sq.next()
                    P.add("act", I("activation", out=sqa, in_=ofa, func=AF.Square), reads=[b_of], writes=[b_sq])
                    pss = bank(P_GATE)
                    sq4 = sqa.rearrange("p (h v t) -> p h v t", h=4, v=2)
                    for v in range(2):
                        P.add("pe", I("matmul", pss.rearrange("p (h t) -> p h t", h=4), lhsT=ones_bf, rhs=sq4[:, :, v, :],
                                      start=(v == 0), stop=(v == 1)),
                              reads=[b_sq, b_ones_bf], writes=([PSB[P_GATE]] if v == 0 else []), pwrites=([] if v == 0 else [PSB[P_GATE]]))
                    rsa, b_rs = rs.next()
                    P.add("dve", I("tensor_scalar", out=rsa, in0=pss, scalar1=1.0 / 256, scalar2=EPS,
                                                                             op0=ALU.mult, op1=ALU.add), reads=[PSB[P_GATE]], writes=[b_rs])
                    P.add("act", I("activation", out=rsa, in_=rsa, func=AF.Ln), reads=[b_rs], writes=[b_rs])
                    P.add("act", I("activation", out=rsa, in_=rsa, func=AF.Exp, scale=-0.5), reads=[b_rs], writes=[b_rs])
                    P.add("dve", I("tensor_tensor",
                        out=ofa.rearrange("p (h v t) -> p h v t", h=4, v=2), in0=ofa.rearrange("p (h v t) -> p h v t", h=4, v=2),
                        in1=rsa.rearrange("p (h t) -> p h t", h=4).unsqueeze(2).to_broadcast([128, 4, 2, 128]), op=ALU.mult),
                        reads=[b_of, b_rs], writes=[b_of])
                    for v in range(2):
                        o4 = ofa.rearrange("p (h v t) -> p h v t", h=4, v=2)[:, :, v, :]
                        z4 = sz3.rearrange("p (h v) t -> p h v t", v=2)[:, :, v, :]
                        m4 = mx3[:, :, tk].rearrange("p (h v) t -> p h v t", v=2)[:, :, v, :]
                        P.add("dve", I("scalar_tensor_tensor",
                            out=m4, in0=o4, scalar=gl[:, v:v + 1], in1=z4, op0=ALU.mult, op1=ALU.mult),
                            reads=[b_of, b_gnorm, b_sz], pwrites=[b_mx])
            if d == 1:
                for hh in range(2):
                    P.dma("pool", OB[s][hh * 512:(hh + 1) * 512, T * 512:(T + 1) * 512].rearrange("(c p) t -> p c t", p=128),
                          os3[:, hh * 4:(hh + 1) * 4, :], b_os, reads=[b_os], pwrites=[b_OB[s]])
            else:
                for hh in range(2):
                    P.dma("pool", MIX[s][1024 + hh * 512:1024 + (hh + 1) * 512, T * 512:(T + 1) * 512].rearrange("(c p) t -> p c t", p=128),
                          mx3[:, hh * 4:(hh + 1) * 4, :], b_mx, reads=[b_mx], pwrites=[b_MIX[s]])
        P.barrier()

    def phaseD(l, s, xsrc, b_xsrc, ydst, b_ydst):
        A.reset()
        wo, b_wo = A.bf16(16 * D, "wo")
        wo3 = wo.rearrange("p (k c) -> p k c", k=16)
        b_won = [Buf(f"wo_n{n}") for n in range(4)]
        for n in range(4):
            for hk in range(4):
                P.dma("sp", wo3[:, hk * 4:(hk + 1) * 4, n * 512:(n + 1) * 512],
                      woutbf[l, hk * 512:(hk + 1) * 512, n * 512:(n + 1) * 512].rearrange("(k p) c -> p k c", p=128),
                      b_won[n], reads=[b_wout[l]], pwrites=[b_won[n]])
        npost, b_np = A.f32(D, "npost")
        P.dma("sp", npost, norm_post[l].partition_broadcast(128), b_np, writes=[b_np])
        mT = Rot([A.bf16(16 * 512, f"mT{i}") for i in range(2)])
        xs = Rot([A.f32(D, f"xd{i}") for i in range(2)])
        tmp = Rot([A.f32(D, f"tmp{i}") for i in range(2)])
        yb = Rot([A.f32(D, f"yb{i}") for i in range(2)])
        junk, b_junk = A.bf16(D, "junkd")
        small = Rot([A.f32(8, f"smd{i}") for i in range(4)])
        pgrp = Rot([0, 4])
        for T in range(ntA):
            ma, b_m = mT.next()
            m3 = ma.rearrange("p (k t) -> p k t", k=16)
            for hk in range(4):
                P.dma("sp", m3[:, hk * 4:(hk + 1) * 4, :],
                      MIX[s][hk * 512:(hk + 1) * 512, T * 512:(T + 1) * 512].rearrange("(k p) t -> p k t", p=128), b_m,
                      reads=[b_MIX[s]], pwrites=[b_m])
            for i in range(4):
                r0 = (T * 4 + i) * 128
                xa, b_x = xs.next()
                P.dma("sp", xa, xsrc[r0:r0 + 128, :], b_x, reads=[b_xsrc[T * 4 + i]], writes=[b_x])
                pg0 = pgrp.next()
                pso = bank(pg0, 4)
                for n in range(4):
                    for kc in range(16):
                        P.add("pe", I("matmul",
                            pso[:, n * 512:(n + 1) * 512], lhsT=m3[:, kc, i * 128:(i + 1) * 128], rhs=wo3[:, kc, n * 512:(n + 1) * 512],
                            start=(kc == 0), stop=(kc == 15)), reads=[b_m, b_won[n]],
                            writes=([PSB[pg0 + n]] if kc == 0 else []), pwrites=([] if kc == 0 else [PSB[pg0 + n]]))
                pbs = [PSB[pg0 + n] for n in range(4)]
                sm, b_sm = small.next()
                P.add("act", I("activation", out=junk, in_=pso, func=AF.Square, accum_out=sm[:, 0:1]),
                      reads=pbs, writes=[b_junk, b_sm])
                P.add("dve", I("tensor_scalar", out=sm[:, 1:2], in0=sm[:, 0:1], scalar1=1.0 / D, scalar2=EPS,
                                                              op0=ALU.mult, op1=ALU.add), reads=[b_sm], writes=[b_sm])
                P.add("act", I("activation", out=sm[:, 3:4], in_=sm[:, 1:2], func=AF.Ln), reads=[b_sm], writes=[b_sm])
                P.add("act", I("activation", out=sm[:, 2:3], in_=sm[:, 3:4], func=AF.Exp, scale=-0.5), reads=[b_sm], writes=[b_sm])
                ta, b_t = tmp.next()
                P.add("dve", I("scalar_tensor_tensor",
                    out=ta, in0=pso, scalar=sm[:, 2:3], in1=npost, op0=ALU.mult, op1=ALU.mult),
                    reads=pbs + [b_sm, b_np], writes=[b_t])
                ya, b_y = yb.next()
                P.add("pool", I("tensor_tensor", out=ya, in0=ta, in1=xa, op=ALU.add),
                      reads=[b_t, b_x], writes=[b_y])
                P.dma("pool", ydst[r0:r0 + 128, :], ya, b_y, reads=[b_y], pwrites=[b_ydst[T * 4 + i]])
        P.barrier()

    b_xin = [[Buf(f"xin{s}_{t}") for t in range(32)] for s in range(NS)]
    if "B" in phases:
        phase0()
    if "W" in phases:
        phaseW()
    for l in range(nlayer):
        for s in range(ns):
            if l == 0:
                xsrc, b_xsrc = x_in[s], b_xin[s]
            else:
                xsrc, b_xsrc = Y1[s], b_Y1[s]
            if l == nlayer - 1 and not cfg.get("y1only", False):
                ydst, b_ydst = y_out[s], b_Y[s]
            else:
                ydst, b_ydst = Y1[s], b_Y1[s]
            if "A" in phases:
                phaseA(l, s, xsrc, b_xsrc)
            if "B" in phases and not (cfg.get("skipB0") and s == 0):
                phaseB(l, s)
            if "C" in phases:
                phaseC(l, s, 1)
                phaseC(l, s, 0)
            if "D" in phases:
                phaseD(l, s, xsrc, b_xsrc, ydst, b_ydst)
    P.final_wait()
    with nc.Block() as block:
        P.emit(block)
    es.close()
    return nc, P


_CACHE = {}


def _in_maps(inputs, cfg=None):
    c = _consts()
    xp = np.asarray(inputs["x_prompt"], np.float32)
    xsamp = np.asarray(inputs["x_sample"], np.float32)
    maps = []
    for core in range(8):
        x = np.stack([xp[core], xsamp[core % 2]], axis=0)
        m = {"x": np.ascontiguousarray(x)}
        for k in ["rel_bias", "w_in", "w_gk_fwd", "w_gk_bwd", "b_gk_fwd", "b_gk_bwd", "sink", "gla_norm", "w_out",
                  "norm_pre", "norm_post"]:
            m[k] = np.ascontiguousarray(np.asarray(inputs[k], np.float32))
        m["c_ident"] = c["ident"]
        m["c_tri"] = c["tri"]
        m["c_oh"] = c["oh"]
        m["c_vm"] = c["vm"]
        m["c_ones_bf"] = c["ones_bf"]
        m["c_ones_f"] = c["ones_f"]
        m["c_J"] = c["J"]
        maps.append(m)
    return maps


def kernel(**inputs):
    if "nc" not in _CACHE:
        _CACHE["nc"] = build()[0]
    nc = _CACHE["nc"]
    maps = _in_maps(inputs)
    res = run_bass_kernel_spmd(nc, maps, core_ids=list(range(8)))
    ys = [r["y"] for r in res.results]
    y_prompt = np.stack([ys[c][0] for c in range(8)], axis=0).astype(np.float32)
    y_sample = np.stack([ys[c][1] for c in range(2)], axis=0).astype(np.float32)
    return (y_prompt, y_sample)
```

```python
import numpy as np
import ml_dtypes
import concourse.bass as bass
import concourse.mybir as mybir
from concourse.bass_utils import run_bass_kernel_spmd

F32 = mybir.dt.float32
BF16 = mybir.dt.bfloat16
AF = mybir.ActivationFunctionType
ALU = mybir.AluOpType

L = 4096
D = 2048
DIN = 5664
NS = 2
NLAYER = 2
EPS = 1e-6
ENGS = ["pe", "act", "dve", "pool", "sp"]
FOLD_WAIT = False
EIDX = {e: i for i, e in enumerate(ENGS)}


class Buf:
    __slots__ = ("name", "w", "r", "war", "sem", "cum", "uid")
    _n = [0]

    def __init__(self, name):
        Buf._n[0] += 1
        self.uid = Buf._n[0]
        self.name = name
        self.w = {}
        self.r = {}
        self.war = {}
        self.sem = None
        self.cum = 0


class Op:
    __slots__ = ("eng", "fn", "deps", "need_inc", "val", "sem", "is_dma", "order", "key", "seq", "clock")


class Prog:
    def __init__(self, nc, es):
        self.nc = nc
        self.es = es
        self.ops = {e: [] for e in ENGS}
        self.esem = {e: es.enter_context(nc.semaphore("sem_" + e)) for e in ENGS}
        self.order = 0
        self.last = {e: None for e in ENGS}
        self.pending_dma = {}
        self.nsem = len(ENGS)
        self.free_sems = []
        self.active_owners = []
        self.eclock = {e: [-1] * len(ENGS) for e in ENGS}
        self.eseq = {e: 0 for e in ENGS}

    def _mk(self, eng, fn, reads, writes, pwrites, owner, extra):
        op = Op()
        op.eng = eng
        op.fn = fn
        op.need_inc = False
        op.val = None
        op.sem = None
        op.is_dma = owner is not None
        self.order += 1
        op.order = self.order
        deps = {}

        ck = self.eclock[eng]

        def dep(o):
            if o is None:
                return
            if (not o.is_dma) and (not op.is_dma) and o.eng == "pe" and eng == "pe":
                return
            if (not o.is_dma) and ck[EIDX[o.eng]] >= o.seq:
                return
            cur = deps.get(o.key)
            if cur is None or o.order > cur.order:
                deps[o.key] = o

        for b in reads:
            for o in b.w.values():
                dep(o)
        for b in list(writes) + list(pwrites):
            if b.r:
                b.war = b.r
                b.r = {}
                b.w = {}
            for o in b.war.values():
                dep(o)
        for b in writes:
            for o in b.w.values():
                dep(o)
        for o in extra:
            dep(o)
        if op.is_dma:
            if owner.sem is None:
                if self.free_sems:
                    owner.sem, owner.cum = self.free_sems.pop()
                else:
                    owner.sem = self.es.enter_context(self.nc.semaphore("dsem%d" % self.nsem))
                    owner.cum = 0
                    self.nsem += 1
                self.active_owners.append(owner)
            owner.cum += 16
            op.sem = owner.sem
            op.val = owner.cum
            op.key = ("d", owner.uid)
            self.pending_dma[op.key] = op
        else:
            op.key = eng
        for o in deps.values():
            o.need_inc = True
            for i2 in range(len(ENGS)):
                if o.clock[i2] > ck[i2]:
                    ck[i2] = o.clock[i2]
            if (not o.is_dma) and o.seq > ck[EIDX[o.eng]]:
                ck[EIDX[o.eng]] = o.seq
        op.seq = self.eseq[eng]
        self.eseq[eng] += 1
        cl = list(ck)
        if not op.is_dma:
            cl[EIDX[eng]] = op.seq
        op.clock = tuple(cl)
        op.deps = list(deps.values())
        for b in reads:
            b.r[op.key] = op
        for b in writes:
            b.w = {op.key: op}
        for b in pwrites:
            b.w[op.key] = op
        self.ops[eng].append(op)
        self.last[eng] = op
        return op

    def add(self, eng, fn, reads=(), writes=(), pwrites=(), extra=()):
        return self._mk(eng, fn, reads, writes, pwrites, None, extra)

    def dma(self, eng, out, in_, owner, reads=(), writes=(), pwrites=(), **kw):
        return self._mk(eng, I("dma_start", out=out, in_=in_, **kw), reads, writes, pwrites, owner, ())

    def barrier(self):
        pend = list(self.pending_dma.values())
        self.pending_dma = {}
        n1 = []
        for e in ENGS:
            extra = list(pend)
            if self.last[e] is not None:
                extra.append(self.last[e])
            n1.append(self._mk(e, lambda en: en.nop(), (), (), (), None, extra))
        for e in ENGS:
            self._mk(e, lambda en: en.nop(), (), (), (), None, n1)
        for b in self.active_owners:
            self.free_sems.append((b.sem, b.cum))
            b.sem = None
        self.active_owners = []

    def final_wait(self):
        pend = list(self.pending_dma.values())
        self._mk("sp", lambda en: en.nop(), (), (), (), None, pend)
        self._mk("sp", lambda en: en.nop(), (), (), (), None, [self.last["sp"]])

    def emit(self, block):
        nc = self.nc
        counts = {}
        for e in ENGS:
            c = 0
            for op in self.ops[e]:
                if op.is_dma:
                    continue
                if op.need_inc:
                    c += 1
                    op.val = c
                    op.sem = self.esem[e]
            counts[e] = (len(self.ops[e]), c)
        self.counts = counts
        self.nwaits = {}

        def run(eng_name):
            def body(E):
                known = {}
                for op in self.ops[eng_name]:
                    need = []
                    for d in op.deps:
                        k = id(d.sem)
                        if known.get(k, 0) < d.val:
                            need.append((d.sem, d.val))
                            known[k] = d.val
                    fold = need.pop() if (need and FOLD_WAIT) else None
                    for (sm, vl) in need:
                        E.wait_ge(sm, vl)
                        self.nwaits[eng_name] = self.nwaits.get(eng_name, 0) + 1
                    ins = op.fn(E)
                    if fold is not None:
                        ins._wait_ge(fold[0], fold[1])
                    if op.is_dma:
                        ins.then_inc(op.sem, 16)
                    elif op.need_inc:
                        ins.then_inc(op.sem, 1)

            return body

        block.tensor(run("pe"))
        block.scalar(run("act"))
        block.vector(run("dve"))
        block.gpsimd(run("pool"))
        block.sync(run("sp"))


def I(name, *a, **k):
    def fn(e):
        return getattr(e, name)(*a, **k)
    return fn


class Arena:
    def __init__(self, ap, nwords):
        self.ap = ap
        self.n = nwords
        self.top = 0
        self.mark = 0

    def f32(self, n, name):
        n = (n + 7) // 8 * 8
        assert self.top + n <= self.n, ("arena overflow", name, self.top, n)
        a = self.ap[:, self.top:self.top + n]
        self.top += n
        return a, Buf(name)

    def bf16(self, n, name):
        w = (n + 1) // 2
        a, b = self.f32(w, name)
        return a.bitcast(BF16)[:, 0:n], b

    def set_mark(self):
        self.mark = self.top

    def reset(self):
        self.top = self.mark


class Rot:
    def __init__(self, items):
        self.items = items
        self.i = 0

    def next(self):
        it = self.items[self.i % len(self.items)]
        self.i += 1
        return it


def _consts():
    c = {}
    c["ident"] = np.eye(128, dtype=np.float32).astype(ml_dtypes.bfloat16)
    jj = np.arange(128)[:, None]
    ii = np.arange(128)[None, :]
    same = (jj // 64) == (ii // 64)
    g = -1.0 / 16.0
    tri = np.zeros((6, 128, 128), np.float32)
    tri[0] = np.where(same & (jj <= ii), g, 0.0)
    tri[1] = np.where(same & (jj >= ii), g, 0.0)
    tri[2] = np.where(same & (jj > ii), g, 0.0)
    tri[3] = np.where(same & (jj < ii), g, 0.0)
    tri[4] = np.where(same & (ii >= jj), 1.0, 0.0)
    tri[5] = np.where(same & (ii <= jj), 1.0, 0.0)
    c["tri"] = np.ascontiguousarray(tri.transpose(1, 0, 2))
    nb = 16
    max_exact = 8
    oh = np.zeros((32, 512), np.float32)
    vm = np.zeros((8, 512), np.float32)
    for idx in range(512):
        rel = 256 - idx
        n = abs(rel)
        if n > 128:
            continue
        if n < max_exact:
            v = n
        else:
            v = max_exact + int(np.log(max(n, 1) / max_exact) / np.log(128 / max_exact) * (nb - max_exact))
            v = min(v, nb - 1)
        b = (nb if rel > 0 else 0) + v
        oh[b, idx] = 1.0
        vm[:, idx] = 1.0
    c["oh"] = oh
    c["vm"] = vm
    c["ones_bf"] = np.ones((128, 128), np.float32).astype(ml_dtypes.bfloat16)
    c["ones_f"] = np.ones((128, 128), np.float32)
    c["J"] = np.ascontiguousarray(np.eye(128, dtype=np.float32)[::-1])
    return c


def _bucket_check():
    qi = np.arange(128)[:, None]
    kj = np.arange(384)[None, :]
    rel = kj - 128 - qi
    nb = 16
    max_exact = 8
    n = np.abs(rel)
    large = max_exact + (np.log(np.maximum(n, 1) / max_exact) / np.log(128 / max_exact) * (nb - max_exact)).astype(np.int32)
    large = np.minimum(large, nb - 1)
    bucket = (rel > 0).astype(np.int32) * nb + np.where(n < max_exact, n, large)
    return bucket, rel


def _blocks():
    blk = {j: {"fm": [], "tm": []} for j in range(12)}
    for c in range(4):
        blk[0]["fm"].append((c * 128, 128, "FB", c * 128, "cp"))
        blk[1]["fm"].append((c * 128, 128, "FB", 512 + c * 128, "cp"))
    blk[2]["fm"].append((0, 128, "FB", 1024, "cp"))
    blk[2]["fm"].append((128, 128, "FB", 1152, "cp"))
    blk[2]["tm"].append((256, 256, "TB", 0))
    for c in range(4):
        blk[3]["fm"].append((c * 128, 128, "FF", c * 128, "silu"))
        blk[4]["fm"].append((c * 128, 128, "FF", 512 + c * 128, "silu"))
        blk[5]["fm"].append((c * 128, 128, "FF", 1024 + c * 128, "cps"))
        blk[6]["fm"].append((c * 128, 128, "FF", 1536 + c * 128, "cp"))
        blk[9]["fm"].append((c * 128, 128, "FF", 2048 + c * 128, "silu"))
        blk[10]["fm"].append((c * 128, 128, "FF", 2560 + c * 128, "silu"))
    blk[6]["tm"].append((0, 512, "TF", 0))
    blk[7]["tm"].append((0, 512, "TB", 256))
    blk[8]["tm"].append((0, 512, "TB", 768))
    blk[11]["fm"].append((0, 32, "FF", 3072, "cp"))
    return blk


def build(cfg=None):
    cfg = dict(cfg or {})
    ns = cfg.get("ns", NS)
    nlayer = cfg.get("nlayer", NLAYER)
    phases = cfg.get("phases", "WABCD")
    ntA = cfg.get("ntA", 8)
    dbg = cfg.get("dbg", False)

    nc = bass.Bass("TRN2", target_bir_lowering=False)
    from contextlib import ExitStack
    es = ExitStack()
    es.enter_context(nc.allow_low_precision("bf16 matmul operands, fp32 accumulation"))

    def din(name, shape, dt=F32):
        return nc.dram_tensor(name, list(shape), dt, kind="ExternalInput").ap()

    def dscr(name, shape, dt, out=False):
        return nc.dram_tensor(name, list(shape), dt, kind=("ExternalOutput" if out else "Internal")).ap()

    x_in = din("x", [NS, L, D])
    rel_bias = din("rel_bias", [32, 8])
    w_in = din("w_in", [NLAYER, D, DIN])
    w_gk = [din("w_gk_fwd", [NLAYER, 16, 512]), din("w_gk_bwd", [NLAYER, 16, 512])]
    b_gk = [din("b_gk_fwd", [NLAYER, 512]), din("b_gk_bwd", [NLAYER, 512])]
    sink = din("sink", [NLAYER, 8])
    gla_norm = din("gla_norm", [NLAYER, 256])
    w_out = din("w_out", [NLAYER, D, D])
    norm_pre = din("norm_pre", [NLAYER, D])
    norm_post = din("norm_post", [NLAYER, D])
    c_ident = din("c_ident", [128, 128], BF16)
    c_tri = din("c_tri", [128, 6, 128])
    c_oh = din("c_oh", [32, 512])
    c_vm = din("c_vm", [8, 512])
    c_ones_bf = din("c_ones_bf", [128, 128], BF16)
    c_ones_f = din("c_ones_f", [128, 128])
    c_J = din("c_J", [128, 128])

    y_out = dscr("y", [NS, L, D], F32, out=True)
    so = dbg
    wbf = dscr("wbf", [NLAYER, D, DIN], BF16, out=False)
    woutbf = dscr("woutbf", [NLAYER, D, D], BF16, out=False)
    def shared(name, shape, dt):
        t = dscr(name + "0", shape, dt, out=so)
        return [t for _ in range(NS)]

    FB = shared("FB", [1280, L], BF16)
    FF = shared("FF", [3104, L], F32)
    TB = shared("TB", [L, 1280], BF16)
    TF = shared("TF", [L, 512], F32)
    MIX = shared("MIX", [D, L], BF16)
    OB = shared("OB", [1024, L], F32)
    if dbg:
        Y1 = [dscr(f"Y1{s}", [L, D], F32, out=so) for s in range(NS)]
    else:
        Y1 = [y_out[s] for s in range(NS)]
    UB = dscr("UB", [8, 512], F32, out=False)

    NW = 52000
    arena_t = es.enter_context(nc.sbuf_tensor("arena", [128, NW], F32))
    psum_t = es.enter_context(nc.psum_tensor("psum", [128, 4096], F32))
    A = Arena(arena_t, NW)
    P = Prog(nc, es)

    def bank(i, n=1):
        return psum_t[:, i * 512:(i + n) * 512]

    PSB = [Buf(f"psb{i}") for i in range(8)]

    b_wbf = [Buf(f"wbf{l}") for l in range(NLAYER)]
    b_wout = [Buf(f"woutbf{l}") for l in range(NLAYER)]
    def sharedb(name):
        b = Buf(name)
        return [b for _ in range(NS)]

    b_FB = sharedb("bFB")
    b_FF = sharedb("bFF")
    b_TB = sharedb("bTB")
    b_TF = sharedb("bTF")
    b_MIX = sharedb("bMIX")
    b_OB = sharedb("bOB")
    b_Y = [[Buf(f"bY{s}_{t}") for t in range(32)] for s in range(NS)]
    b_Y1 = [[Buf(f"bY1{s}_{t}") for t in range(32)] for s in range(NS)] if dbg else b_Y
    b_UB = Buf("bUB")

    ident, b_ident = A.bf16(128, "ident")
    tri, b_tri = A.f32(6 * 128, "tri")
    tri3 = tri.rearrange("p (a b) -> p a b", a=6)
    ones_bf, b_ones_bf = A.bf16(128, "ones_bf")
    ones_f, b_ones_f = A.f32(128, "ones_f")
    expbT, b_expbT = A.f32(3 * 8 * 128, "expbT")
    expbT4 = expbT.rearrange("p (o h q) -> p o h q", o=3, h=8)
    esink, b_esink = A.f32(8 * NLAYER, "esink")
    gnorm, b_gnorm = A.f32(2 * NLAYER, "gnorm")
    P.dma("sp", ident, c_ident, b_ident, writes=[b_ident])
    P.dma("sp", tri3, c_tri, b_tri, writes=[b_tri])
    P.dma("sp", ones_bf, c_ones_bf, b_ones_bf, writes=[b_ones_bf])
    P.dma("sp", ones_f, c_ones_f, b_ones_f, writes=[b_ones_f])
    for l in range(NLAYER):
        P.dma("sp", esink[:, l * 8:(l + 1) * 8], sink[l].partition_broadcast(128), b_esink, pwrites=[b_esink])
        for v in range(2):
            P.dma("sp", gnorm[:, l * 2 + v:l * 2 + v + 1],
                  gla_norm[l, v * 128:(v + 1) * 128].rearrange("(p o) -> p o", o=1), b_gnorm, pwrites=[b_gnorm])
    P.add("act", I("activation", out=esink, in_=esink, func=AF.Exp), reads=[b_esink], writes=[b_esink])
    A.set_mark()

    def phase0():
        A.reset()
        oh, b_oh = A.f32(512, "oh")
        vm, b_vm = A.f32(512, "vm")
        rb, b_rb = A.f32(8, "rb")
        u, b_u = A.f32(512, "u")
        P.dma("sp", oh[0:32, :], c_oh, b_oh, writes=[b_oh])
        P.dma("sp", vm[0:8, :], c_vm, b_vm, writes=[b_vm])
        P.dma("sp", rb[0:32, :], rel_bias, b_rb, writes=[b_rb])
        ps = bank(0)
        P.add("pe", I("matmul", ps[0:8, :], lhsT=rb[0:32, 0:8], rhs=oh[0:32, :], start=True, stop=True),
              reads=[b_rb, b_oh], writes=[PSB[0]])
        P.add("act", I("activation", out=u[0:8, :], in_=ps[0:8, :], func=AF.Exp), reads=[PSB[0]], writes=[b_u])
        P.add("dve", I("tensor_tensor", out=u[0:8, :], in0=u[0:8, :], in1=vm[0:8, :], op=ALU.mult),
              reads=[b_u, b_vm], writes=[b_u])
        P.dma("sp", UB, u[0:8, :], b_u, reads=[b_u], writes=[b_UB])
        ubt = UB.tensor
        jm, b_jm = A.f32(128, "jm")
        P.dma("sp", jm, c_J, b_jm, writes=[b_jm])
        for o in range(3):
            wt, b_wt = A.f32(1024, f"wt{o}")
            for hh in range(2):
                src_ap = bass.AP(tensor=ubt, offset=129 - (o - 1) * 128 + hh * 4 * 512, ap=[[1, 128], [512, 4], [1, 128]])
                P.dma("sp", wt.rearrange("p (h q) -> p h q", h=8)[:, hh * 4:(hh + 1) * 4, :], src_ap, b_wt, reads=[b_UB], pwrites=[b_wt])
            for hf in range(2):
                pj = bank(1 + hf)
                P.add("pe", I("matmul", pj, lhsT=jm, rhs=wt[:, hf * 512:(hf + 1) * 512], start=True, stop=True),
                      reads=[b_jm, b_wt], writes=[PSB[1 + hf]])
                P.add("dve", I("tensor_copy", out=expbT4[:, o, hf * 4:(hf + 1) * 4, :],
                               in_=pj.rearrange("p (h q) -> p h q", h=4)), reads=[PSB[1 + hf]], pwrites=[b_expbT])
        P.barrier()

    def phaseW():
        lanes = [Buf(f"lane{i}") for i in range(4)]
        k = 0
        for l in range(nlayer):
            for r in range(8):
                src = w_in[l, r * 256:(r + 1) * 256, :].rearrange("r (a b) -> r a b", a=3)
                dst = wbf[l, r * 256:(r + 1) * 256, :].rearrange("r (a b) -> r a b", a=3)
                P.dma("pool", dst, src, lanes[k % 4], writes=[lanes[k % 4]], pwrites=[b_wbf[l]])
                k += 1
            for r in range(8):
                src = w_out[l, r * 256:(r + 1) * 256, :]
                dst = woutbf[l, r * 256:(r + 1) * 256, :]
                P.dma("pool", dst, src, lanes[k % 4], writes=[lanes[k % 4]], pwrites=[b_wout[l]])
                k += 1

    BLK = _blocks()

    def phaseA(l, s, xsrc, b_xsrc):
        A.reset()
        npre, b_npre = A.f32(D, "npre")
        P.dma("sp", npre, norm_pre[l].partition_broadcast(128), b_npre, writes=[b_npre])
        xs = Rot([A.f32(D, f"xs{i}") for i in range(2)])
        junk, b_junk = A.bf16(D, "junk")
        hb = Rot([A.bf16(D, f"hb{i}") for i in range(2)])
        hT = Rot([A.bf16(16 * 512, f"hT{i}") for i in range(2)])
        wb = Rot([A.bf16(16 * 512, f"wb{i}") for i in range(3)])
        stg = Rot([A.f32(512, f"stg{i}") for i in range(6)])
        small = Rot([A.f32(8, f"sm{i}") for i in range(4)])
        pt = psum_t[:, 0:1024].bitcast(BF16)
        po = Rot([2, 3, 4, 5, 6, 7])
        dst_ap = {"FB": FB[s], "FF": FF[s], "TB": TB[s], "TF": TF[s]}
        dst_buf = {"FB": b_FB[s], "FF": b_FF[s], "TB": b_TB[s], "TF": b_TF[s]}
        ncopy = [0]

        def evac(psap, mode, dt_bf, shape_p, n):
            sa, sb = stg.next()
            if dt_bf:
                o = sa.bitcast(BF16)[0:shape_p, 0:n]
            else:
                o = sa[0:shape_p, 0:n]
            return o, sb

        hT_of = {}

        def front_ew(T, i):
            if T not in hT_of:
                hTa, b_hT = hT.next()
                hT_of[T] = (hTa.rearrange("p (k t) -> p k t", k=16), b_hT)
            xa, b_x = xs.next()
            r0 = (T * 4 + i) * 128
            P.dma("sp", xa, xsrc[r0:r0 + 128, :], b_x, reads=[b_xsrc[T * 4 + i]], writes=[b_x])
            sm, b_sm = small.next()
            P.add("act", I("activation", out=junk, in_=xa, func=AF.Square, accum_out=sm[:, 0:1]),
                  reads=[b_x], writes=[b_junk, b_sm])
            P.add("dve", I("tensor_scalar", out=sm[:, 1:2], in0=sm[:, 0:1], scalar1=1.0 / D, scalar2=EPS,
                           op0=ALU.mult, op1=ALU.add), reads=[b_sm], writes=[b_sm])
            P.add("act", I("activation", out=sm[:, 3:4], in_=sm[:, 1:2], func=AF.Ln), reads=[b_sm], writes=[b_sm])
            P.add("act", I("activation", out=sm[:, 2:3], in_=sm[:, 3:4], func=AF.Exp, scale=-0.5), reads=[b_sm], writes=[b_sm])
            ha, b_h = hb.next()
            P.add("dve", I("scalar_tensor_tensor", out=ha, in0=xa, scalar=sm[:, 2:3], in1=npre, op0=ALU.mult, op1=ALU.mult),
                  reads=[b_x, b_sm, b_npre], writes=[b_h])
            return (T, i, ha, b_h)

        def front_pe(ctx):
            T, i, ha, b_h = ctx
            hT3, b_hT = hT_of[T]
            for kc in range(16):
                P.add("pe", I("transpose", pt[:, kc * 128:(kc + 1) * 128], ha[:, kc * 128:(kc + 1) * 128], ident),
                      reads=[b_h, b_ident], pwrites=[PSB[kc // 8]])
            for hf in range(2):
                src_ = pt[:, hf * 1024:(hf + 1) * 1024].rearrange("p (k t) -> p k t", k=8)
                dstv = hT3[:, hf * 8:(hf + 1) * 8, i * 128:(i + 1) * 128]
                if hf == 0:
                    P.add("act", I("copy", out=dstv, in_=src_), reads=[PSB[hf]], pwrites=[b_hT])
                else:
                    P.add("dve", I("tensor_copy", out=dstv, in_=src_), reads=[PSB[hf]], pwrites=[b_hT])

        def groups(T):
            hT3, b_hT = hT_of[T]
            for j in range(12):
                wa, b_w = wb.next()
                w3 = wa.rearrange("p (k c) -> p k c", k=16)
                ncol = 512 if j < 11 else 32
                for hk in range(4):
                    P.dma("sp", w3[:, hk * 4:(hk + 1) * 4, 0:ncol],
                          wbf[l, hk * 512:(hk + 1) * 512, j * 512:j * 512 + ncol].rearrange("(k p) c -> p k c", p=128),
                          b_w, reads=[b_wbf[l]], pwrites=[b_w])
                for (c0, ncl, dname, drow, mode) in BLK[j]["fm"]:
                    pb = po.next()
                    ps = bank(pb)
                    for kc in range(16):
                        P.add("pe", I("matmul", ps[0:ncl, :], lhsT=w3[:, kc, c0:c0 + ncl], rhs=hT3[:, kc, :], start=(kc == 0), stop=(kc == 15)),
                            reads=[b_w, b_hT], writes=([PSB[pb]] if kc == 0 else []), pwrites=([] if kc == 0 else [PSB[pb]]))
                    isbf = (dname == "FB")
                    o, b_o = evac(ps, mode, isbf, ncl, 512)
                    if mode == "silu":
                        P.add("act", I("activation", out=o, in_=ps[0:ncl, :], func=AF.Silu), reads=[PSB[pb]], writes=[b_o])
                    elif mode == "cps":
                        P.add("act", I("mul", out=o, in_=ps[0:ncl, :], mul=128 ** -0.5), reads=[PSB[pb]], writes=[b_o])
                    else:
                        ncopy[0] += 1
                        if ncopy[0] % 3 == 0:
                            P.add("act", I("copy", out=o, in_=ps[0:ncl, :]), reads=[PSB[pb]], writes=[b_o])
                        else:
                            P.add("dve", I("tensor_copy", out=o, in_=ps[0:ncl, :]), reads=[PSB[pb]], writes=[b_o])
                    P.dma("pool", dst_ap[dname][drow:drow + ncl, T * 512:(T + 1) * 512], o, b_o, reads=[b_o], pwrites=[dst_buf[dname]])
                    yield
                for (c0, ncl, dname, dcol) in BLK[j]["tm"]:
                    for i in range(4):
                        pb = po.next()
                        ps = bank(pb)
                        for kc in range(16):
                            P.add("pe", I("matmul", ps[:, 0:ncl], lhsT=hT3[:, kc, i * 128:(i + 1) * 128], rhs=w3[:, kc, c0:c0 + ncl],
                                start=(kc == 0), stop=(kc == 15)),
                                reads=[b_w, b_hT], writes=([PSB[pb]] if kc == 0 else []), pwrites=([] if kc == 0 else [PSB[pb]]))
                        isbf = (dname == "TB")
                        o, b_o = evac(ps, "cp", isbf, 128, ncl)
                        ncopy[0] += 1
                        if ncopy[0] % 3 == 0:
                            P.add("act", I("copy", out=o, in_=ps[:, 0:ncl]), reads=[PSB[pb]], writes=[b_o])
                        else:
                            P.add("dve", I("tensor_copy", out=o, in_=ps[:, 0:ncl]), reads=[PSB[pb]], writes=[b_o])
                        r0 = (T * 4 + i) * 128
                        P.dma("pool", dst_ap[dname][r0:r0 + 128, dcol:dcol + ncl], o, b_o, reads=[b_o], pwrites=[dst_buf[dname]])
                        yield

        for i in range(4):
            front_pe(front_ew(0, i))
        for T in range(ntA):
            pend = {}
            g = 0
            for _ in groups(T):
                g += 1
                if T + 1 < ntA:
                    for i in range(4):
                        if g == 4 + 11 * i:
                            pend[i] = front_ew(T + 1, i)
                        if g == 9 + 11 * i:
                            front_pe(pend.pop(i))
            assert not pend
        P.barrier()

    def phaseB(l, s):
        A.reset()
        kT, b_kT = A.bf16(2 * L, "kT")
        kT3 = kT.rearrange("p (g t) -> p g t", g=2)
        V, b_V = A.bf16(32 * 256, "V")
        V3 = V.rearrange("p (b c) -> p b c", b=32)
        P.dma("sp", kT3, FB[s][1024:1280, :].rearrange("(g p) t -> p g t", p=128), b_kT, reads=[b_FB[s]], writes=[b_kT])
        for part in range(8):
            P.dma("sp", V3[:, part * 4:(part + 1) * 4, :],
                  TB[s][part * 512:(part + 1) * 512, 0:256].rearrange("(b p) c -> p b c", p=128), b_V,
                  reads=[b_TB[s]], pwrites=[b_V])
        qT = Rot([A.bf16(8 * 512, f"qT{i}") for i in range(2)])
        sza = Rot([A.f32(8 * 512, f"sza{i}") for i in range(2)])
        atb = Rot([A.bf16(8 * 512, f"atb{i}") for i in range(2)])
        ef = Rot([A.f32(512, f"ef{i}") for i in range(4)])
        pT = Rot([A.bf16(512, f"pT{i}") for i in range(6)])
        rec = Rot([A.f32(512, f"rec{i}") for i in range(2)])
        at = Rot([A.f32(512, f"at{i}") for i in range(2)])
        stb = Rot([0, 1, 2, 3])
        otb = Rot([4, 5])
        dnb = Rot([6, 7])
        scale = 128 ** -0.5
        es_l = esink[:, l * 8:(l + 1) * 8]
        esrow, b_esrow = A.bf16(1024, "esrow")
        P.add("dve", I("tensor_copy", out=esrow[0:1, :].rearrange("p (h q) -> p h q", h=8),
                       in_=es_l[0:1, :].unsqueeze(2).to_broadcast([1, 8, 128])), reads=[b_esink], writes=[b_esrow])
        for T in range(8):
            qa, b_q = qT.next()
            q3 = qa.rearrange("p (h t) -> p h t", h=8)
            za, b_z = sza.next()
            z3 = za.rearrange("p (h t) -> p h t", h=8)
            aa, b_a = atb.next()
            a3 = aa.rearrange("p (h t) -> p h t", h=8)
            for hh in range(2):
                P.dma("sp", q3[:, hh * 4:(hh + 1) * 4, :],
                      FB[s][hh * 512:(hh + 1) * 512, T * 512:(T + 1) * 512].rearrange("(h p) t -> p h t", p=128), b_q,
                      reads=[b_FB[s]], pwrites=[b_q])
                P.dma("sp", z3[:, hh * 4:(hh + 1) * 4, :],
                      FF[s][hh * 512:(hh + 1) * 512, T * 512:(T + 1) * 512].rearrange("(h p) t -> p h t", p=128), b_z,
                      reads=[b_FF[s]], pwrites=[b_z])
            units = [(ib, g) for ib in range(4) for g in range(2)]
            pend = None

            def front(ib, g):
                qi = T * 4 + ib
                offs = [o for o in (-1, 0, 1) if 0 <= qi + o < 32]
                pts = []
                for o in offs:
                    kb = qi + o
                    sbk = stb.next()
                    st = bank(sbk)
                    P.add("pe", I("matmul",
                        st, lhsT=kT3[:, g, kb * 128:(kb + 1) * 128], rhs=q3[:, 4 * g:4 * g + 4, ib * 128:(ib + 1) * 128],
                        start=True, stop=True), reads=[b_kT, b_q], writes=[PSB[sbk]])
                    ea, b_e = ef.next()
                    P.add("act", I("activation", out=ea, in_=st, func=AF.Exp, scale=scale),
                          reads=[PSB[sbk]], writes=[b_e])
                    pa, b_p = pT.next()
                    P.add(("dve" if o >= 0 else "pool"), I("tensor_tensor",
                        out=pa.rearrange("p (h q) -> p h q", h=4), in0=ea.rearrange("p (h q) -> p h q", h=4),
                        in1=expbT4[:, o + 1, 4 * g:4 * g + 4, :], op=ALU.mult), reads=[b_e, b_expbT], writes=[b_p])
                    pts.append((kb, pa, b_p))
                return (ib, g, pts)

            def back(u):
                ib, g, pts = u
                ob_ = otb.next()
                db_ = dnb.next()
                oT = bank(ob_)
                dn = bank(db_)
                n = len(pts)
                for k, (kb, pa, b_p) in enumerate(pts):
                    P.add("pe", I("matmul", oT, lhsT=V3[:, kb, g * 128:(g + 1) * 128], rhs=pa, start=(k == 0), stop=(k == n - 1)),
                        reads=[b_V, b_p], writes=([PSB[ob_]] if k == 0 else []), pwrites=([] if k == 0 else [PSB[ob_]]))
                P.add("pe", I("matmul", dn, lhsT=ones_bf[0:1, :], rhs=esrow[0:1, g * 512:(g + 1) * 512], start=True, stop=False),
                      reads=[b_ones_bf, b_esrow], writes=[PSB[db_]])
                for k, (kb, pa, b_p) in enumerate(pts):
                    P.add("pe", I("matmul", dn, lhsT=ones_bf, rhs=pa, start=False, stop=(k == n - 1)),
                        reads=[b_ones_bf, b_p], pwrites=[PSB[db_]])
                ra, b_r = rec.next()
                P.add("act", I("activation", out=ra, in_=dn, func=AF.Ln), reads=[PSB[db_]], writes=[b_r])
                P.add("act", I("activation", out=ra, in_=ra, func=AF.Exp, scale=-1.0), reads=[b_r], writes=[b_r])
                ta, b_t = at.next()
                P.add("pool", I("tensor_tensor", out=ta.rearrange("p (h q) -> p h q", h=4), in0=ra.rearrange("p (h q) -> p h q", h=4),
                    in1=z3[:, 4 * g:4 * g + 4, ib * 128:(ib + 1) * 128], op=ALU.mult), reads=[b_r, b_z], writes=[b_t])
                P.add("dve", I("tensor_tensor", out=a3[:, 4 * g:4 * g + 4, ib * 128:(ib + 1) * 128],
                    in0=oT.rearrange("p (h q) -> p h q", h=4), in1=ta.rearrange("p (h q) -> p h q", h=4), op=ALU.mult),
                    reads=[PSB[ob_], b_t], pwrites=[b_a])

            for (ib, g) in units:
                u = front(ib, g)
                if pend is not None:
                    back(pend)
                pend = u
            back(pend)
            for hh in range(2):
                P.dma("pool", MIX[s][hh * 512:(hh + 1) * 512, T * 512:(T + 1) * 512].rearrange("(h p) t -> p h t", p=128),
                      a3[:, hh * 4:(hh + 1) * 4, :], b_a, reads=[b_a], pwrites=[b_MIX[s]])
        P.barrier()

    def phaseC(l, s, d):
        A.reset()
        wg, b_wg = A.f32(512, "wg")
        P.dma("sp", wg[0:16, :], w_gk[d][l], b_wg, pwrites=[b_wg])
        P.dma("sp", wg[16:17, :], b_gk[d][l:l + 1, :], b_wg, pwrites=[b_wg])
        wgh, b_wgh = A.bf16(512, "wgh")
        wgl, b_wgl = A.bf16(512, "wgl")
        P.add("dve", I("tensor_copy", out=wgh[0:17, :], in_=wg[0:17, :]), reads=[b_wg], writes=[b_wgh])
        P.add("dve", I("tensor_tensor", out=wgl[0:17, :], in0=wg[0:17, :], in1=wgh[0:17, :], op=ALU.subtract),
              reads=[b_wg, b_wgh], writes=[b_wgl])
        trib, b_trib = A.bf16(4 * 128, "trib")
        trib3 = trib.rearrange("p (a b) -> p a b", a=4)
        P.add("dve", I("tensor_copy", out=trib3, in_=tri3[:, 0:4, :]), reads=[b_tri], writes=[b_trib])
        lrh = Rot([A.bf16(512, f"lrh{i}") for i in range(2)])
        lrl = Rot([A.bf16(512, f"lrl{i}") for i in range(2)])
        sph = Rot([A.bf16(512, f"sph{i}") for i in range(3)])
        spl = Rot([A.bf16(512, f"spl{i}") for i in range(3)])
        lr = Rot([A.f32(512, f"lr{i}") for i in range(2)])
        for (a_, b_) in lr.items:
            P.add("pool", I("memset", a_[0:17, :], 1.0), writes=[b_])
        Sst, _ = A.f32(1024, "S")
        S3 = Sst.rearrange("p (h v) -> p h v", h=4)
        b_Sh = [Buf(f"S{h}") for h in range(4)]
        P.add("pool", I("memset", Sst, 0.0), writes=b_Sh)
        Sb = []
        for i in range(2):
            a_, _b = A.bf16(1024, f"Sb{i}")
            Sb.append((a_.rearrange("p (h v) -> p h v", h=4), [Buf(f"Sb{i}_{hf}") for hf in range(2)]))
        P.add("pool", I("memset", Sb[0][0], 0.0), writes=Sb[0][1])
        sb_state = {"i": 0}
        qbT = Rot([A.f32(4 * 512, f"qbT{i}") for i in range(2)])
        kbT = Rot([A.f32(4 * 512, f"kbT{i}") for i in range(2)])
        kbk = Rot([A.f32(4 * 512, f"kbk{i}") for i in range(2)])
        vbk = Rot([A.bf16(4 * 1024, f"vbk{i}") for i in range(2)])
        if d == 0:
            obl = Rot([A.f32(8 * 128, f"obl{i}") for i in range(2)])
            szb = Rot([A.f32(8 * 128, f"szb{i}") for i in range(2)])
            mixb = Rot([A.bf16(8 * 512, f"mixb{i}") for i in range(2)])
        else:
            obs = Rot([A.f32(8 * 512, f"obs{i}") for i in range(2)])
        e1 = Rot([A.f32(512, f"e1{i}") for i in range(3)])
        spb = Rot([A.f32(512, f"sp{i}") for i in range(3)])
        EbT = Rot([A.f32(512, f"EbT{i}") for i in range(3)])
        EnbT = Rot([A.f32(512, f"EnbT{i}") for i in range(2)])
        Etl = Rot([A.f32(512, f"Etl{i}") for i in range(2)])
        qd = Rot([A.bf16(512, f"qd{i}") for i in range(3)])
        kd = Rot([A.bf16(512, f"kd{i}") for i in range(2)])
        ktl = Rot([A.bf16(512, f"ktl{i}") for i in range(3)])
        ATb = Rot([A.bf16(512, f"ATb{i}") for i in range(3)])
        of = Rot([A.f32(1024, f"of{i}") for i in range(2)])
        sq = Rot([A.bf16(1024, f"sq{i}") for i in range(2)])
        rs = Rot([A.f32(512, f"rs{i}") for i in range(2)])
        TRI = trib3[:, (1 if d else 0), :]
        TT = trib3[:, (3 if d else 2), :]
        MSK = tri3[:, (5 if d else 4), :]
        P_F1, P_F2, P_OT, P_KV, P_SS, P_G = 0, 1, 2, 4, 6, 7
        gl = gnorm[:, l * 2:(l + 1) * 2]
        dk_scale = 128 ** -0.5
        torder = list(range(8)) if d == 0 else list(range(7, -1, -1))
        iorder = list(range(4)) if d == 0 else list(range(3, -1, -1))
        corder = [0, 1] if d == 0 else [1, 0]
        sup = {}

        def load_super(T):
            lra, b_lr = lr.next()
            P.dma("sp", lra[0:16, :], FF[s][3072 + 16 * d:3072 + 16 * d + 16, T * 512:(T + 1) * 512], b_lr,
                  reads=[b_FF[s]], pwrites=[b_lr])
            qa, b_q = qbT.next()
            q3 = qa.rearrange("p (h t) -> p h t", h=4)
            P.dma("sp", q3, FF[s][1024:1536, T * 512:(T + 1) * 512].rearrange("(h p) t -> p h t", p=128), b_q,
                  reads=[b_FF[s]], writes=[b_q])
            ka, b_k = kbT.next()
            k3 = ka.rearrange("p (h t) -> p h t", h=4)
            P.dma("sp", k3, FF[s][1536:2048, T * 512:(T + 1) * 512].rearrange("(h p) t -> p h t", p=128), b_k,
                  reads=[b_FF[s]], writes=[b_k])
            kka, b_kk = kbk.next()
            kk3 = kka.rearrange("p (i c) -> p i c", i=4)
            P.dma("sp", kk3, TF[s][T * 512:(T + 1) * 512, :].rearrange("(i p) c -> p i c", p=128), b_kk,
                  reads=[b_TF[s]], writes=[b_kk])
            va, b_v = vbk.next()
            v3 = va.rearrange("p (i c) -> p i c", i=4)
            P.dma("sp", v3, TB[s][T * 512:(T + 1) * 512, 256:1280].rearrange("(i p) c -> p i c", p=128), b_v,
                  reads=[b_TB[s]], writes=[b_v])
            lha, b_lh = lrh.next()
            lla, b_ll = lrl.next()
            P.add("act", I("copy", out=lha[0:17, :], in_=lra[0:17, :]), reads=[b_lr], writes=[b_lh])
            P.add("pool", I("tensor_tensor", out=lla[0:17, :], in0=lra[0:17, :], in1=lha[0:17, :], op=ALU.subtract),
                  reads=[b_lr, b_lh], writes=[b_ll])
            c = dict(lha=lha, b_lh=b_lh, lla=lla, b_ll=b_ll, q3=q3, b_q=b_q, k3=k3, b_k=b_k, kk3=kk3, b_kk=b_kk, v3=v3, b_v=b_v)
            if d == 0:
                mxa, b_mx = mixb.next()
                c["mx3"] = mxa.rearrange("p (c t) -> p c t", c=8)
                c["b_mx"] = b_mx
            else:
                osa, b_os = obs.next()
                c["os3"] = osa.rearrange("p (c t) -> p c t", c=8)
                c["b_os"] = b_os
            return c

        def front_a(T, i):
            if T not in sup:
                sup[T] = load_super(T)
            S_ = sup[T]
            c = dict(S_)
            c["T"], c["i"] = T, i
            tk = slice(i * 128, (i + 1) * 128)
            c["tk"] = tk
            pg = bank(P_G)
            P.add("pe", I("matmul", pg, lhsT=S_["lha"][0:17, tk], rhs=wgh[0:17, :], start=True, stop=False),
                  reads=[S_["b_lh"], b_wgh], writes=[PSB[P_G]])
            P.add("pe", I("matmul", pg, lhsT=S_["lla"][0:17, tk], rhs=wgh[0:17, :], start=False, stop=False),
                  reads=[S_["b_ll"], b_wgh], pwrites=[PSB[P_G]])
            P.add("pe", I("matmul", pg, lhsT=S_["lha"][0:17, tk], rhs=wgl[0:17, :], start=False, stop=True),
                  reads=[S_["b_lh"], b_wgl], pwrites=[PSB[P_G]])
            e1a, b_e1 = e1.next()
            P.add("act", I("activation", out=e1a, in_=pg, func=AF.Exp, scale=-1.0), reads=[PSB[P_G]], writes=[b_e1])
            spa, b_sp = spb.next()
            P.add("act", I("activation", out=spa, in_=e1a, func=AF.Ln, bias=1.0), reads=[b_e1], writes=[b_sp])
            sha, b_sh = sph.next()
            sla, b_sl = spl.next()
            P.add("act", I("copy", out=sha, in_=spa), reads=[b_sp], writes=[b_sh])
            P.add("pool", I("tensor_tensor", out=sla, in0=spa, in1=sha, op=ALU.subtract), reads=[b_sp, b_sh], writes=[b_sl])
            c.update(sha=sha, b_sh=b_sh, sla=sla, b_sl=b_sl)
            return c

        def front_b(c):
            S_ = c
            T, i, tk = c["T"], c["i"], c["tk"]
            sha, b_sh, sla, b_sl = c["sha"], c["b_sh"], c["sla"], c["b_sl"]
            pbt = bank(P_F2)
            for h in range(4):
                P.add("pe", I("matmul", pbt[:, h * 128:(h + 1) * 128], lhsT=sha[:, h * 128:(h + 1) * 128], rhs=TRI,
                              start=(h == 0), stop=False, skip_group_check=True), reads=[b_sh, b_trib],
                      writes=([PSB[P_F2]] if h == 0 else []), pwrites=([] if h == 0 else [PSB[P_F2]]))
            for h in range(4):
                P.add("pe", I("matmul", pbt[:, h * 128:(h + 1) * 128], lhsT=sla[:, h * 128:(h + 1) * 128], rhs=TRI,
                              start=False, stop=True, skip_group_check=True), reads=[b_sl, b_trib], pwrites=[PSB[P_F2]])
            ptl = bank(P_F1)
            P.add("pe", I("matmul", ptl, lhsT=TT, rhs=sha, start=True, stop=False), reads=[b_sh, b_trib], writes=[PSB[P_F1]])
            P.add("pe", I("matmul", ptl, lhsT=TT, rhs=sla, start=False, stop=True), reads=[b_sl, b_trib], pwrites=[PSB[P_F1]])
            eb, b_eb = EbT.next()
            enb, b_enb = EnbT.next()
            etl, b_etl = Etl.next()
            P.add("act", I("activation", out=eb, in_=pbt, func=AF.Exp), reads=[PSB[P_F2]], writes=[b_eb])
            P.add("act", I("activation", out=enb, in_=pbt, func=AF.Exp, scale=-1.0), reads=[PSB[P_F2]], writes=[b_enb])
            P.add("act", I("activation", out=etl, in_=ptl, func=AF.Exp), reads=[PSB[P_F1]], writes=[b_etl])
            eb3 = eb.rearrange("p (h t) -> p h t", h=4)
            qda, b_qd = qd.next()
            qd3 = qda.rearrange("p (h t) -> p h t", h=4)
            P.add("dve", I("tensor_tensor", out=qd3, in0=S_["q3"][:, :, tk], in1=eb3, op=ALU.mult),
                  reads=[S_["b_q"], b_eb], writes=[b_qd])
            kda, b_kd = kd.next()
            kd3 = kda.rearrange("p (h t) -> p h t", h=4)
            P.add("dve", I("tensor_tensor", out=kd3, in0=S_["k3"][:, :, tk], in1=enb.rearrange("p (h t) -> p h t", h=4),
                            op=ALU.mult), reads=[S_["b_k"], b_enb], writes=[b_kd])
            kta, b_kt = ktl.next()
            P.add("pool", I("tensor_tensor", out=kta, in0=S_["kk3"][:, i, :], in1=etl, op=ALU.mult),
                  reads=[S_["b_kk"], b_etl], writes=[b_kt])
            c.update(eb3=eb3, b_eb=b_eb, qd3=qd3, b_qd=b_qd, kta=kta, b_kt=b_kt, kd3=kd3, b_kd=b_kd)
            return c

        def front_b2(c):
            kd3, b_kd, qd3, b_qd = c["kd3"], c["b_kd"], c["qd3"], c["b_qd"]
            pat = bank(P_F2)
            for h in range(4):
                P.add("pe", I("matmul", pat[:, h * 128:(h + 1) * 128], lhsT=kd3[:, h, :], rhs=qd3[:, h, :], start=True, stop=True),
                      reads=[b_kd, b_qd], writes=([PSB[P_F2]] if h == 0 else []), pwrites=([] if h == 0 else [PSB[P_F2]]))
            aba, b_ab = ATb.next()
            ab3 = aba.rearrange("p (h t) -> p h t", h=4)
            P.add("dve", I("tensor_tensor", out=ab3, in0=pat.rearrange("p (h t) -> p h t", h=4),
                           in1=MSK.unsqueeze(1).to_broadcast([128, 4, 128]), op=ALU.mult),
                  reads=[PSB[P_F2], b_tri], writes=[b_ab])
            c.update(ab3=ab3, b_ab=b_ab)
            return c

        def back(c):
            T, i, tk = c["T"], c["i"], c["tk"]
            v3, b_v, ab3, b_ab, qd3, b_qd = c["v3"], c["b_v"], c["ab3"], c["b_ab"], c["qd3"], c["b_qd"]
            kta, b_kt, eb3, b_eb = c["kta"], c["b_kt"], c["eb3"], c["b_eb"]
            if d == 0:
                t0 = T * 512 + i * 128
                oba, b_ob = obl.next()
                ob3 = oba.rearrange("p (c t) -> p c t", c=8)
                sza_, b_sz = szb.next()
                sz3 = sza_.rearrange("p (c t) -> p c t", c=8)
                for hh in range(2):
                    P.dma("sp", ob3[:, hh * 4:(hh + 1) * 4, :],
                          OB[s][hh * 512:(hh + 1) * 512, t0:t0 + 128].rearrange("(c p) t -> p c t", p=128), b_ob,
                          reads=[b_OB[s]], pwrites=[b_ob])
                    P.dma("sp", sz3[:, hh * 4:(hh + 1) * 4, :],
                          FF[s][2048 + hh * 512:2048 + (hh + 1) * 512, t0:t0 + 128].rearrange("(c p) t -> p c t", p=128), b_sz,
                          reads=[b_FF[s]], pwrites=[b_sz])
            pot = bank(P_OT, 2)
            first_in_bank = {0: True, 1: True}
            for h in range(4):
                for v in range(2):
                    c8 = h * 2 + v
                    bk = c8 // 4
                    P.add("pe", I("matmul", pot[:, c8 * 128:(c8 + 1) * 128],
                                  lhsT=v3[:, i, h * 256 + v * 128:h * 256 + (v + 1) * 128], rhs=ab3[:, h, :],
                                  start=first_in_bank[bk], stop=False, skip_group_check=True),
                          reads=[b_v, b_ab], writes=([PSB[P_OT + bk]] if first_in_bank[bk] else []),
                          pwrites=([] if first_in_bank[bk] else [PSB[P_OT + bk]]))
                    first_in_bank[bk] = False
            for ci, cch in enumerate(corder):
                cs = slice(cch * 64, (cch + 1) * 64)
                sbv, b_sbh = Sb[sb_state["i"] % 2]
                sbn, b_sbn = Sb[(sb_state["i"] + 1) % 2]
                sb_state["i"] += 1
                if d == 0:
                    dcol = 63 if cch == 0 else 127
                else:
                    dcol = 64 if cch == 1 else 0
                for hf in range(2):
                    for h in (2 * hf, 2 * hf + 1):
                        for v in range(2):
                            c8 = h * 2 + v
                            bk = c8 // 4
                            P.add("pe", I("matmul", pot[:, c8 * 128 + cch * 64:c8 * 128 + (cch + 1) * 64],
                                          lhsT=sbv[:, h, v * 128:(v + 1) * 128], rhs=qd3[:, h, cch * 64:(cch + 1) * 64],
                                          start=False, stop=(ci == 1), skip_group_check=True),
                                  reads=[b_sbh[hf], b_qd], pwrites=[PSB[P_OT + bk]])
                    pkv = bank(P_KV + hf)
                    for hh, h in enumerate((2 * hf, 2 * hf + 1)):
                        P.add("pe", I("matmul", pkv[:, hh * 256:(hh + 1) * 256], lhsT=kta[cs, h * 128:(h + 1) * 128],
                                      rhs=v3[cs, i, h * 256:(h + 1) * 256], start=True, stop=True),
                              reads=[b_kt, b_v], writes=([PSB[P_KV + hf]] if hh == 0 else []),
                              pwrites=([] if hh == 0 else [PSB[P_KV + hf]]))
                    for hh, h in enumerate((2 * hf, 2 * hf + 1)):
                        P.add("dve", I("scalar_tensor_tensor", out=S3[:, h, :], in0=S3[:, h, :],
                                       scalar=eb3[:, h, dcol:dcol + 1], in1=pkv[:, hh * 256:(hh + 1) * 256],
                                       op0=ALU.mult, op1=ALU.add),
                              reads=[b_Sh[h], b_eb, PSB[P_KV + hf]], writes=[b_Sh[h]])
                    P.add("dve", I("tensor_copy", out=sbn[:, 2 * hf:2 * hf + 2, :], in_=S3[:, 2 * hf:2 * hf + 2, :]),
                          reads=[b_Sh[2 * hf], b_Sh[2 * hf + 1]], writes=[b_sbn[hf]])
            if d == 1:
                P.add("act", I("copy", out=c["os3"][:, :, tk], in_=pot.rearrange("p (c t) -> p c t", c=8)),
                      reads=[PSB[P_OT], PSB[P_OT + 1]], pwrites=[c["b_os"]])
            else:
                mx3, b_mx = c["mx3"], c["b_mx"]
                ofa, b_of = of.next()
                of3 = ofa.rearrange("p (c t) -> p c t", c=8)
                P.add("dve", I("tensor_tensor", out=of3, in0=pot.rearrange("p (c t) -> p c t", c=8), in1=ob3, op=ALU.add),
                      reads=[PSB[P_OT], PSB[P_OT + 1], b_ob], writes=[b_of])
                sqa, b_sq = sq.next()
                P.add("act", I("activation", out=sqa, in_=ofa, func=AF.Square), reads=[b_of], writes=[b_sq])
                pss = bank(P_SS)
                sq4 = sqa.rearrange("p (h v t) -> p h v t", h=4, v=2)
                for v in range(2):
                    P.add("pe", I("matmul", pss.rearrange("p (h t) -> p h t", h=4), lhsT=ones_bf, rhs=sq4[:, :, v, :],
                                  start=(v == 0), stop=(v == 1)),
                          reads=[b_sq, b_ones_bf], writes=([PSB[P_SS]] if v == 0 else []), pwrites=([] if v == 0 else [PSB[P_SS]]))
                rsa, b_rs = rs.next()
                P.add("act", I("activation", out=rsa, in_=pss, func=AF.Ln, scale=1.0 / 256, bias=EPS), reads=[PSB[P_SS]], writes=[b_rs])
                P.add("act", I("activation", out=rsa, in_=rsa, func=AF.Exp, scale=-0.5), reads=[b_rs], writes=[b_rs])
                P.add("dve", I("tensor_tensor", out=ofa.rearrange("p (h v t) -> p h v t", h=4, v=2),
                               in0=ofa.rearrange("p (h v t) -> p h v t", h=4, v=2),
                               in1=rsa.rearrange("p (h t) -> p h t", h=4).unsqueeze(2).to_broadcast([128, 4, 2, 128]), op=ALU.mult),
                      reads=[b_of, b_rs], writes=[b_of])
                for v in range(2):
                    o4 = ofa.rearrange("p (h v t) -> p h v t", h=4, v=2)[:, :, v, :]
                    z4 = sz3.rearrange("p (h v) t -> p h v t", v=2)[:, :, v, :]
                    m4 = mx3[:, :, tk].rearrange("p (h v) t -> p h v t", v=2)[:, :, v, :]
                    P.add("dve", I("scalar_tensor_tensor", out=m4, in0=o4, scalar=gl[:, v:v + 1], in1=z4, op0=ALU.mult, op1=ALU.mult),
                          reads=[b_of, b_gnorm, b_sz], pwrites=[b_mx])
            if i == iorder[-1]:
                if d == 1:
                    for hh in range(2):
                        P.dma("pool", OB[s][hh * 512:(hh + 1) * 512, T * 512:(T + 1) * 512].rearrange("(c p) t -> p c t", p=128),
                              c["os3"][:, hh * 4:(hh + 1) * 4, :], c["b_os"], reads=[c["b_os"]], pwrites=[b_OB[s]])
                else:
                    for hh in range(2):
                        P.dma("pool", MIX[s][1024 + hh * 512:1024 + (hh + 1) * 512, T * 512:(T + 1) * 512].rearrange("(c p) t -> p c t", p=128),
                              c["mx3"][:, hh * 4:(hh + 1) * 4, :], c["b_mx"], reads=[c["b_mx"]], pwrites=[b_MIX[s]])

        tiles = [(T, i) for T in torder for i in iorder]
        n = len(tiles)
        fa, fb = {}, {}
        for idx in range(n + 2):
            if idx < n:
                fa[idx] = front_a(*tiles[idx])
            if 1 <= idx <= n:
                fb[idx - 1] = front_b(fa.pop(idx - 1))
            if idx >= 2:
                back(fb.pop(idx - 2))
            if 1 <= idx <= n:
                fb[idx - 1] = front_b2(fb[idx - 1])
        P.barrier()

    def phaseD(l, s, xsrc, b_xsrc, ydst, b_ydst):
        A.reset()
        wo, b_wo = A.bf16(16 * D, "wo")
        wo3 = wo.rearrange("p (k c) -> p k c", k=16)
        b_won = [Buf(f"wo_n{n}") for n in range(4)]
        for n in range(4):
            for hk in range(4):
                P.dma("sp", wo3[:, hk * 4:(hk + 1) * 4, n * 512:(n + 1) * 512],
                      woutbf[l, hk * 512:(hk + 1) * 512, n * 512:(n + 1) * 512].rearrange("(k p) c -> p k c", p=128),
                      b_won[n], reads=[b_wout[l]], pwrites=[b_won[n]])
        npost, b_np = A.f32(D, "npost")
        P.dma("sp", npost, norm_post[l].partition_broadcast(128), b_np, writes=[b_np])
        mT = Rot([A.bf16(16 * 512, f"mT{i}") for i in range(2)])
        xs = Rot([A.f32(D, f"xd{i}") for i in range(2)])
        tmp = Rot([A.f32(D, f"tmp{i}") for i in range(2)])
        yb = Rot([A.f32(D, f"yb{i}") for i in range(2)])
        junk, b_junk = A.bf16(D, "junkd")
        small = Rot([A.f32(8, f"smd{i}") for i in range(4)])
        pgrp = Rot([0, 4])
        for T in range(ntA):
            ma, b_m = mT.next()
            m3 = ma.rearrange("p (k t) -> p k t", k=16)
            for hk in range(4):
                P.dma("sp", m3[:, hk * 4:(hk + 1) * 4, :],
                      MIX[s][hk * 512:(hk + 1) * 512, T * 512:(T + 1) * 512].rearrange("(k p) t -> p k t", p=128), b_m,
                      reads=[b_MIX[s]], pwrites=[b_m])
            for i in range(4):
                r0 = (T * 4 + i) * 128
                xa, b_x = xs.next()
                P.dma("sp", xa, xsrc[r0:r0 + 128, :], b_x, reads=[b_xsrc[T * 4 + i]], writes=[b_x])
                pg0 = pgrp.next()
                pso = bank(pg0, 4)
                for n in range(4):
                    for kc in range(16):
                        P.add("pe", I("matmul",
                            pso[:, n * 512:(n + 1) * 512], lhsT=m3[:, kc, i * 128:(i + 1) * 128], rhs=wo3[:, kc, n * 512:(n + 1) * 512],
                            start=(kc == 0), stop=(kc == 15)), reads=[b_m, b_won[n]],
                            writes=([PSB[pg0 + n]] if kc == 0 else []), pwrites=([] if kc == 0 else [PSB[pg0 + n]]))
                pbs = [PSB[pg0 + n] for n in range(4)]
                sm, b_sm = small.next()
                P.add("act", I("activation", out=junk, in_=pso, func=AF.Square, accum_out=sm[:, 0:1]),
                      reads=pbs, writes=[b_junk, b_sm])
                P.add("dve", I("tensor_scalar", out=sm[:, 1:2], in0=sm[:, 0:1], scalar1=1.0 / D, scalar2=EPS,
                                                              op0=ALU.mult, op1=ALU.add), reads=[b_sm], writes=[b_sm])
                P.add("act", I("activation", out=sm[:, 3:4], in_=sm[:, 1:2], func=AF.Ln), reads=[b_sm], writes=[b_sm])
                P.add("act", I("activation", out=sm[:, 2:3], in_=sm[:, 3:4], func=AF.Exp, scale=-0.5), reads=[b_sm], writes=[b_sm])
                ta, b_t = tmp.next()
                P.add("dve", I("scalar_tensor_tensor",
                    out=ta, in0=pso, scalar=sm[:, 2:3], in1=npost, op0=ALU.mult, op1=ALU.mult),
                    reads=pbs + [b_sm, b_np], writes=[b_t])
                ya, b_y = yb.next()
                P.add("pool", I("tensor_tensor", out=ya, in0=ta, in1=xa, op=ALU.add),
                      reads=[b_t, b_x], writes=[b_y])
                P.dma("pool", ydst[r0:r0 + 128, :], ya, b_y, reads=[b_y], pwrites=[b_ydst[T * 4 + i]])
        P.barrier()

    b_xin = [[Buf(f"xin{s}_{t}") for t in range(32)] for s in range(NS)]
    if "B" in phases:
        phase0()
    if "W" in phases:
        phaseW()
    for l in range(nlayer):
        for s in range(ns):
            if l == 0:
                xsrc, b_xsrc = x_in[s], b_xin[s]
            else:
                xsrc, b_xsrc = Y1[s], b_Y1[s]
            if l == nlayer - 1 and not cfg.get("y1only", False):
                ydst, b_ydst = y_out[s], b_Y[s]
            else:
                ydst, b_ydst = Y1[s], b_Y1[s]
            if "A" in phases:
                phaseA(l, s, xsrc, b_xsrc)
            if "B" in phases and not (cfg.get("skipB0") and s == 0):
                phaseB(l, s)
            if "C" in phases:
                phaseC(l, s, 1)
                phaseC(l, s, 0)
            if "D" in phases:
                phaseD(l, s, xsrc, b_xsrc, ydst, b_ydst)
    P.final_wait()
    with nc.Block() as block:
        P.emit(block)
    es.close()
    return nc, P


_CACHE = {}


def _in_maps(inputs, cfg=None):
    c = _consts()
    xp = np.asarray(inputs["x_prompt"], np.float32)
    xsamp = np.asarray(inputs["x_sample"], np.float32)
    maps = []
    for core in range(8):
        x = np.stack([xp[core], xsamp[core % 2]], axis=0)
        m = {"x": np.ascontiguousarray(x)}
        for k in ["rel_bias", "w_in", "w_gk_fwd", "w_gk_bwd", "b_gk_fwd", "b_gk_bwd", "sink", "gla_norm", "w_out",
                  "norm_pre", "norm_post"]:
            m[k] = np.ascontiguousarray(np.asarray(inputs[k], np.float32))
        m["c_ident"] = c["ident"]
        m["c_tri"] = c["tri"]
        m["c_oh"] = c["oh"]
        m["c_vm"] = c["vm"]
        m["c_ones_bf"] = c["ones_bf"]
        m["c_ones_f"] = c["ones_f"]
        m["c_J"] = c["J"]
        maps.append(m)
    return maps


def kernel(**inputs):
    if "nc" not in _CACHE:
        _CACHE["nc"] = build()[0]
    nc = _CACHE["nc"]
    maps = _in_maps(inputs)
    res = run_bass_kernel_spmd(nc, maps, core_ids=list(range(8)))
    ys = [r["y"] for r in res.results]
    y_prompt = np.stack([ys[c][0] for c in range(8)], axis=0).astype(np.float32)
    y_sample = np.stack([ys[c][1] for c in range(2)], axis=0).astype(np.float32)
    return (y_prompt, y_sample)
```
